# Optimizing a Trainium2 kernel written in Bass

```python
import jax, jax.numpy as jnp
from jax import lax
import numpy as np

D_MODEL = 1024
BATCH = 4
SEQ = 8192
DEPTH = 1

CHUNK = 64
GMLP_BLOCK = 128
GMLP_GROUP_DIM = 128
GMLP_WIDTH = D_MODEL
GMLP_GROUPS = GMLP_WIDTH // GMLP_GROUP_DIM
S5_GROUP_DIM = 16
S5_WIDTH = D_MODEL // 2
S5_GROUPS = S5_WIDTH // S5_GROUP_DIM
S5_STATE = 64
N_BRANCHES = 2
IN_WIDTH = 2 * GMLP_WIDTH + S5_WIDTH + N_BRANCHES * D_MODEL
FFN_HIDDEN = -(-8 * D_MODEL // 768) * 256
DT_MIN = 1e-3
DT_MAX = 1e-1
EPS = 1e-6

kernel_name = "hybrid_gmlp_s5_gated_streaming_block"


def rms_norm(x, g):
    xf = x.astype(jnp.float32)
    y = xf * lax.rsqrt(jnp.mean(xf * xf, axis=-1, keepdims=True) + EPS)
    return (y * g.astype(jnp.float32)).astype(x.dtype)


def layer_norm(x, g, b):
    xf = x.astype(jnp.float32)
    mu = jnp.mean(xf, axis=-1, keepdims=True)
    xc = xf - mu
    y = xc * lax.rsqrt(jnp.mean(xc * xc, axis=-1, keepdims=True) + EPS)
    return (y * g.astype(jnp.float32) + b.astype(jnp.float32)).astype(x.dtype)


def gmlp_mixer(u, v, ln_g, ln_b, ws, bs):
    bsz, length, _ = u.shape
    u = jax.nn.gelu(u)
    v = layer_norm(jax.nn.gelu(v), ln_g, ln_b)
    cidx = jnp.arange(GMLP_BLOCK) // CHUNK
    mask = cidx[None, :] <= cidx[:, None]
    ws_m = jnp.where(mask[None], ws, jnp.zeros_like(ws))
    vb = v.reshape(bsz, length // GMLP_BLOCK, GMLP_BLOCK, GMLP_GROUPS, GMLP_GROUP_DIM)
    mixed = jnp.einsum('gij,bnjgc->bnigc', ws_m, vb) + bs.T[:, :, None]
    return u * mixed.reshape(bsz, length, GMLP_WIDTH)


def _ssm_combine(e1, e2):
    a1r, a1i, b1r, b1i = e1
    a2r, a2i, b2r, b2i = e2
    return (a2r * a1r - a2i * a1i,
            a2r * a1i + a2i * a1r,
            a2r * b1r - a2i * b1i + b2r,
            a2r * b1i + a2i * b1r + b2i)


def s5_mixer(xb, lam_re, lam_im, log_dt, b_re, b_im, c_re, c_im, d, w_glu, b_glu):
    dtype = xb.dtype
    f32 = jnp.float32
    bsz, length, _ = xb.shape
    dt = jnp.exp(log_dt.astype(f32))[:, None]
    lr = lam_re.astype(f32)
    li = lam_im.astype(f32)
    mag = jnp.exp(lr * dt)
    ab_re = mag * jnp.cos(li * dt)
    ab_im = mag * jnp.sin(li * dt)
    den = lr * lr + li * li
    nr = ab_re - 1.0
    coef_re = (nr * lr + ab_im * li) / den
    coef_im = (ab_im * lr - nr * li) / den
    br = b_re.astype(f32)
    bi = b_im.astype(f32)
    bb_re = coef_re[..., None] * br - coef_im[..., None] * bi
    bb_im = coef_re[..., None] * bi + coef_im[..., None] * br
    cr = c_re.astype(f32)
    ci = c_im.astype(f32)
    df = d.astype(f32)

    u = xb.astype(f32).reshape(bsz, length // CHUNK, CHUNK, S5_GROUPS, S5_GROUP_DIM)
    u = u.transpose(1, 0, 2, 3, 4)

    def step(carry, u_c):
        h_re, h_im = carry
        bu_re = jnp.einsum('bcgh,gph->bcgp', u_c, bb_re)
        bu_im = jnp.einsum('bcgh,gph->bcgp', u_c, bb_im)
        a_re = jnp.broadcast_to(ab_re, bu_re.shape)
        a_im = jnp.broadcast_to(ab_im, bu_im.shape)
        pr, pim, xr, xi = lax.associative_scan(_ssm_combine, (a_re, a_im, bu_re, bu_im), axis=1)
        xr = xr + pr * h_re[:, None] - pim * h_im[:, None]
        xi = xi + pr * h_im[:, None] + pim * h_re[:, None]
        y = (jnp.einsum('bcgp,ghp->bcgh', xr, cr)
             - jnp.einsum('bcgp,ghp->bcgh', xi, ci)
             + df * u_c)
        return (xr[:, -1], xi[:, -1]), y

    h0 = jnp.zeros((bsz, S5_GROUPS, S5_STATE), f32)
    _, ys = lax.scan(step, (h0, h0), u)
    y = ys.transpose(1, 0, 2, 3, 4).reshape(bsz, length, S5_WIDTH)
    y = jax.nn.gelu(y)
    y = y * jax.nn.sigmoid(y @ w_glu.astype(f32) + b_glu.astype(f32))
    return y.astype(dtype)


def hybrid_layer(x, norm1_g, w_in, gmlp_ln_g, gmlp_ln_b, gmlp_ws, gmlp_bs,
                 s5_lambda_re, s5_lambda_im, s5_log_dt, s5_b_re, s5_b_im, s5_c_re, s5_c_im,
                 s5_d, s5_w_glu, s5_b_glu, w_branch_a, w_branch_b, w_out,
                 norm2_g, w_ffn_gate, w_ffn_up, w_ffn_down):
    h = rms_norm(x, norm1_g)
    proj = h @ w_in
    s1 = GMLP_WIDTH
    s2 = 2 * GMLP_WIDTH
    s3 = s2 + S5_WIDTH
    s4 = s3 + D_MODEL
    u_a = proj[..., :s1]
    v_a = proj[..., s1:s2]
    x_b = proj[..., s2:s3]
    g_a = proj[..., s3:s4]
    g_b = proj[..., s4:]
    y_a = gmlp_mixer(u_a, v_a, gmlp_ln_g, gmlp_ln_b, gmlp_ws, gmlp_bs)
    y_b = s5_mixer(x_b, s5_lambda_re, s5_lambda_im, s5_log_dt, s5_b_re, s5_b_im,
                   s5_c_re, s5_c_im, s5_d, s5_w_glu, s5_b_glu)
    merged = jax.nn.sigmoid(g_a) * (y_a @ w_branch_a) + jax.nn.sigmoid(g_b) * (y_b @ w_branch_b)
    x = x + merged @ w_out
    h2 = rms_norm(x, norm2_g)
    x = x + (jax.nn.silu(h2 @ w_ffn_gate) * (h2 @ w_ffn_up)) @ w_ffn_down
    return x


def setup_inputs(seed: int = 0) -> dict:
    key = jax.random.key(seed)
    ks = jax.random.split(key, 32)
    f32 = jnp.float32
    L = DEPTH
    G, P, H = S5_GROUPS, S5_STATE, S5_GROUP_DIM
    nrm = lambda k, shape, s: jax.random.normal(k, shape, f32) * s
    log_dt = jax.random.uniform(ks[9], (L, G), f32, np.log(DT_MIN), np.log(DT_MAX))
    return {
        "x": jax.random.normal(ks[0], (BATCH, SEQ, D_MODEL), f32),
        "norm1_g": 1.0 + nrm(ks[1], (L, D_MODEL), 0.01),
        "w_in": nrm(ks[2], (L, D_MODEL, IN_WIDTH), D_MODEL ** -0.5),
        "gmlp_ln_g": 1.0 + nrm(ks[3], (L, GMLP_WIDTH), 0.01),
        "gmlp_ln_b": nrm(ks[4], (L, GMLP_WIDTH), 0.01),
        "gmlp_ws": nrm(ks[5], (L, GMLP_GROUPS, GMLP_BLOCK, GMLP_BLOCK), GMLP_BLOCK ** -0.5),
        "gmlp_bs": 1.0 + nrm(ks[6], (L, GMLP_GROUPS, GMLP_BLOCK), 0.01),
        "s5_lambda_re": -0.5 + nrm(ks[7], (L, G, P), 0.01),
        "s5_lambda_im": jnp.broadcast_to(np.pi * jnp.arange(P, dtype=f32), (L, G, P)) + nrm(ks[8], (L, G, P), 0.01),
        "s5_log_dt": log_dt,
        "s5_b_re": nrm(ks[10], (L, G, P, H), (2.0 * H) ** -0.5),
        "s5_b_im": nrm(ks[11], (L, G, P, H), (2.0 * H) ** -0.5),
        "s5_c_re": nrm(ks[12], (L, G, H, P), (2.0 * P) ** -0.5),
        "s5_c_im": nrm(ks[13], (L, G, H, P), (2.0 * P) ** -0.5),
        "s5_d": nrm(ks[14], (L, G, H), 1.0),
        "s5_w_glu": nrm(ks[15], (L, S5_WIDTH, S5_WIDTH), S5_WIDTH ** -0.5),
        "s5_b_glu": nrm(ks[16], (L, S5_WIDTH), 0.01),
        "w_branch_a": nrm(ks[17], (L, GMLP_WIDTH, D_MODEL), GMLP_WIDTH ** -0.5),
        "w_branch_b": nrm(ks[18], (L, S5_WIDTH, D_MODEL), S5_WIDTH ** -0.5),
        "w_out": nrm(ks[19], (L, D_MODEL, D_MODEL), D_MODEL ** -0.5),
        "norm2_g": 1.0 + nrm(ks[20], (L, D_MODEL), 0.01),
        "w_ffn_gate": nrm(ks[21], (L, D_MODEL, FFN_HIDDEN), D_MODEL ** -0.5),
        "w_ffn_up": nrm(ks[22], (L, D_MODEL, FFN_HIDDEN), D_MODEL ** -0.5),
        "w_ffn_down": nrm(ks[23], (L, FFN_HIDDEN, D_MODEL), FFN_HIDDEN ** -0.5),
        "norm_f_g": 1.0 + nrm(ks[24], (D_MODEL,), 0.01),
    }


def reference(x, norm1_g, w_in, gmlp_ln_g, gmlp_ln_b, gmlp_ws, gmlp_bs,
              s5_lambda_re, s5_lambda_im, s5_log_dt, s5_b_re, s5_b_im, s5_c_re, s5_c_im,
              s5_d, s5_w_glu, s5_b_glu, w_branch_a, w_branch_b, w_out,
              norm2_g, w_ffn_gate, w_ffn_up, w_ffn_down, norm_f_g):
    for l in range(DEPTH):
        x = hybrid_layer(x, norm1_g[l], w_in[l], gmlp_ln_g[l], gmlp_ln_b[l], gmlp_ws[l], gmlp_bs[l],
                         s5_lambda_re[l], s5_lambda_im[l], s5_log_dt[l], s5_b_re[l], s5_b_im[l],
                         s5_c_re[l], s5_c_im[l], s5_d[l], s5_w_glu[l], s5_b_glu[l],
                         w_branch_a[l], w_branch_b[l], w_out[l],
                         norm2_g[l], w_ffn_gate[l], w_ffn_up[l], w_ffn_down[l])
    return rms_norm(x, norm_f_g)
```

```python
import os
import numpy as np
import ml_dtypes
import concourse.bass as bass
import concourse.mybir as mybir
from concourse.bass_utils import run_bass_kernel_spmd

F32 = mybir.dt.float32
BF16 = mybir.dt.bfloat16
AF = mybir.ActivationFunctionType
ALU = mybir.AluOpType

D = 1024
NTOK = 4096
FF = 2816
NF = 22
EPS = 1e-6
S5_ON = os.environ.get("K_S5", "1") == "1"
NTILES = int(os.environ.get("K_NTILES", "8"))
TWO_PI = 2.0 * np.pi


class Tok:
    __slots__ = ("sem", "v")

    def __init__(self, sem, v):
        self.sem, self.v = sem, v


class Buf:
    __slots__ = ("w", "r")

    def __init__(self):
        self.w = []
        self.r = []


class Eng:
    def __init__(self, name, nsem, inc, ops=None, seen=None):
        self.name = name
        self.ops = [] if ops is None else ops
        self.seen = {} if seen is None else seen
        self.inc = inc
        self.sems = [f"{name}{i}" for i in range(nsem)]
        self.vals = [0] * nsem
        self.last = [None] * nsem
        self.slot = 0

    def wait(self, tok):
        if tok is None or self.seen.get(tok.sem, 0) >= tok.v:
            return
        self.seen[tok.sem] = tok.v
        self.ops.append(("w", tok.sem, tok.v))

    def emit(self, fn):
        i = self.slot
        self.slot = (self.slot + 1) % len(self.sems)
        if len(self.sems) > 1 and self.last[i] is not None:
            self.wait(self.last[i])
        self.vals[i] += self.inc
        tok = Tok(self.sems[i], self.vals[i])
        self.ops.append(("o", fn, self.sems[i], self.inc))
        self.last[i] = tok
        return tok

    def emit_nosig(self, fn):
        self.ops.append(("o", fn, None, 0))


def run(E, fns, reads=(), writes=(), acc=False, nowait=False):
    if not isinstance(fns, (list, tuple)):
        fns = [fns]
    for b in reads:
        for t in b.w:
            E.wait(t)
    for b in writes:
        if nowait:
            continue
        for t in b.w:
            E.wait(t)
        for t in b.r:
            E.wait(t)
    for f in fns[:-1]:
        E.emit_nosig(f)
    tok = E.emit(fns[-1])
    for b in reads:
        b.r.append(tok)
        if len(b.r) > 64:
            b.r = b.r[-64:]
    for b in writes:
        if acc:
            b.w.append(tok)
        else:
            b.w = [tok]
            b.r = []
    return tok


def build_program():
    nc = bass.Bass("TRN2", target_bir_lowering=False)

    def din(name, shape, dt=F32):
        return nc.dram_tensor(name, list(shape), dt, kind="ExternalInput").ap()

    x_own = din("x_own", [NTOK, D])
    x_prev = din("x_prev", [NTOK, D])
    norm1_g = din("norm1_g", [D])
    w_in = din("w_in", [D, 4608])
    gmlp_ln_g = din("gmlp_ln_g", [D])
    gmlp_ln_b = din("gmlp_ln_b", [D])
    gmlp_ws = din("gmlp_ws", [8, 128, 128])
    gmlp_bs = din("gmlp_bs", [8, 128])
    lam_re = din("s5_lambda_re", [32, 64])
    lam_im = din("s5_lambda_im", [32, 64])
    log_dt = din("s5_log_dt", [32])
    b_re = din("s5_b_re", [32, 64, 16])
    b_im = din("s5_b_im", [32, 64, 16])
    c_re = din("s5_c_re", [32, 16, 64])
    c_im = din("s5_c_im", [32, 16, 64])
    s5_d = din("s5_d", [32, 16])
    w_glu = din("s5_w_glu", [512, 512])
    b_glu = din("s5_b_glu", [512])
    w_a = din("w_branch_a", [D, D])
    w_b = din("w_branch_b", [512, D])
    w_out = din("w_out", [D, D])
    norm2_g = din("norm2_g", [D])
    w_fg = din("w_ffn_gate", [D, FF])
    w_fu = din("w_ffn_up", [D, FF])
    w_fd = din("w_ffn_down", [FF, D])
    norm_f_g = din("norm_f_g", [D])
    c_ident_bf = din("c_ident_bf", [128, 128], BF16)
    c_ident_f = din("c_ident_f", [128, 128])
    c_mask_toep = din("c_mask_toep", [128, 128])
    c_mask_ws = din("c_mask_ws", [128, 128])
    c_swap_bf = din("c_swap_bf", [128, 128], BF16)
    c_swap_f = din("c_swap_f", [128, 128])
    out = nc.dram_tensor("out", [NTOK, D], F32, kind="ExternalOutput").ap()
    DBG = os.environ.get("K_DEBUG", "0") == "1"
    if DBG:
        dbg_yb = nc.dram_tensor("dbg_yb", [128, 16384], BF16, kind="ExternalOutput").ap()
        dbg_h1 = nc.dram_tensor("dbg_h1", [128, 16384], BF16, kind="ExternalOutput").ap()
        dbg_u = nc.dram_tensor("dbg_u", [128, 4096], BF16, kind="ExternalOutput").ap()
        dbg_ys = nc.dram_tensor("dbg_ys", [128, 4096], BF16, kind="ExternalOutput").ap()
        dbg_ycm = nc.dram_tensor("dbg_ycm", [128, 4096], BF16, kind="ExternalOutput").ap()
        dbg_yg = nc.dram_tensor("dbg_yg", [128, 4096], BF16, kind="ExternalOutput").ap()

    win_d = nc.dram_tensor("win_d", [128, 8, 4608], BF16).ap()
    wa_d = nc.dram_tensor("wa_d", [128, 8, D], BF16).ap()
    wb_d = nc.dram_tensor("wb_d", [128, 4, D], BF16).ap()
    wo_d = nc.dram_tensor("wo_d", [128, 8, D], BF16).ap()
    wglu_d = nc.dram_tensor("wglu_d", [128, 4, 512], BF16).ap()
    wfg_d = nc.dram_tensor("wfg_d", [128, 8, FF], BF16).ap()
    wfu_d = nc.dram_tensor("wfu_d", [128, 8, FF], BF16).ap()
    wfd_d = nc.dram_tensor("wfd_d", [128, 8, NF, 128], BF16).ap()

    PE = Eng("pe", 1, 1)
    ACT = Eng("act", 1, 1)
    DVE = Eng("dve", 1, 1)
    POOL = Eng("pool", 1, 1)
    PQ = Eng("pq", 12, 16, ops=POOL.ops, seen=POOL.seen)
    SQ = Eng("sq", 24, 16)
    all_sem_names = PE.sems + ACT.sems + DVE.sems + POOL.sems + PQ.sems + SQ.sems

    ARENA_BYTES = 212800
    arena_cm = nc.sbuf_tensor("arena", [128, ARENA_BYTES // 2], BF16)
    arena = arena_cm.__enter__()
    psf_cm = nc.psum_tensor("psf", [128, 6, 512], F32)
    psf = psf_cm.__enter__()
    psb_cm = nc.psum_tensor("psb", [128, 2, 1024], BF16)
    psb = psb_cm.__enter__()

    class Alloc:
        def __init__(self, base=0):
            self.off = base

        def get(self, shape, dt):
            n = int(np.prod(shape))
            nb = n * (4 if dt == F32 else 2)
            nb = (nb + 63) // 64 * 64
            assert self.off + nb <= ARENA_BYTES, ("arena overflow", self.off + nb)
            a = arena[:, self.off // 2:(self.off + nb) // 2]
            self.off += nb
            if dt == F32:
                a = a.bitcast(F32)
            a = a[:, 0:n]
            if len(shape) == 2:
                return a.rearrange("p (a b) -> p a b", b=shape[1])
            if len(shape) == 3:
                return a.rearrange("p (a b c) -> p a b c", b=shape[1], c=shape[2])
            return a

    PSF = [Buf() for _ in range(6)]
    PSB = [Buf() for _ in range(2)]

    AL = Alloc(0)
    yb = AL.get([4, 4, 1024], BF16)
    YB = [Buf() for _ in range(4)]
    ident_bf = AL.get([128], BF16)
    ident_f = AL.get([128], F32)
    B_const = Buf()
    persist_end = AL.off

    run(PQ, lambda e: e.dma_start(out=ident_bf, in_=c_ident_bf[:, :]), writes=[B_const], acc=True)
    run(PQ, lambda e: e.dma_start(out=ident_f, in_=c_ident_f[:, :]), writes=[B_const], acc=True)

    CONV = {}

    def conv(key, dst, src, ktiles):
        CONV[key] = Buf()
        for k in range(ktiles):
            run(PQ, lambda e, k=k: e.dma_start(out=dst[:, k, :], in_=src[k * 128:(k + 1) * 128, :]),
                writes=[CONV[key]], acc=True, nowait=True)

    CONV["winB"] = Buf()
    for k in range(8):
        run(PQ, lambda e, k=k: e.dma_start(out=win_d[:, k, 2048:2560], in_=w_in[k * 128:(k + 1) * 128, 2048:2560]),
            writes=[CONV["winB"]], acc=True, nowait=True)

    conv_parts = []
    def conv_win():
        CONV["win"] = Buf()
        for k in range(8):
            for c0, c1 in ((0, 2048), (2560, 4608)):
                run(PQ, lambda e, k=k, c0=c0, c1=c1: e.dma_start(out=win_d[:, k, c0:c1],
                                                                 in_=w_in[k * 128:(k + 1) * 128, c0:c1]),
                    writes=[CONV["win"]], acc=True, nowait=True)
    conv_parts.append(conv_win)
    if S5_ON:
        conv_parts.append(lambda: conv("wglu", wglu_d, w_glu, 4))
    conv_parts.append(lambda: conv("wa", wa_d, w_a, 8))
    conv_parts.append(lambda: (conv("wb", wb_d, w_b, 4), conv("wo", wo_d, w_out, 8)))
    conv_parts.append(lambda: conv("wfg", wfg_d, w_fg, 8))
    conv_parts.append(lambda: conv("wfu", wfu_d, w_fu, 8))

    def conv_wfd():
        CONV["wfd"] = Buf()
        for f in range(NF):
            run(PQ, lambda e, f=f: e.dma_start(
                out=wfd_d[:, :, f, :], in_=w_fd[f * 128:(f + 1) * 128, :].rearrange("p (m c) -> p m c", c=128)),
                writes=[CONV["wfd"]], acc=True, nowait=True)
    conv_parts.append(conv_wfd)

    def bulk_conv(n=None):
        k = len(conv_parts) if n is None else n
        for _ in range(k):
            if conv_parts:
                conv_parts.pop(0)()

    H1B = Buf()

    AL = Alloc(persist_end)
    gcols = AL.get([40], F32)
    cst2 = AL.get([4], F32)
    epsc = cst2[:, 0:1]
    B_setup = Buf()
    g1T, g2T, lngT, lnbT, bgluT = gcols[:, 0:8], gcols[:, 8:16], gcols[:, 16:24], gcols[:, 24:32], gcols[:, 32:36]
    gstage = Alloc(200 * 1024).get([128], F32)
    for r0, nk, src in ((0, 8, norm1_g), (8, 8, norm2_g), (16, 8, gmlp_ln_g), (24, 8, gmlp_ln_b), (32, 4, b_glu)):
        run(SQ, lambda e, r0=r0, nk=nk, src=src: e.dma_start(out=gstage[r0:r0 + nk, :],
                                                             in_=src.rearrange("(k p) -> k p", p=128)),
            writes=[B_setup], acc=True)
    run(PE, lambda e: e.transpose(out=psf[:, 0, 0:36], in_=gstage[0:36, :], identity=ident_f[0:36, 0:36]),
        reads=[B_setup, B_const], writes=[PSF[0]])
    run(DVE, lambda e: e.tensor_copy(out=gcols[:, 0:36], in_=psf[:, 0, 0:36]), reads=[PSF[0]], writes=[B_setup], acc=True)
    run(POOL, lambda e: e.memset(epsc, EPS), writes=[B_setup], acc=True)
    persist_end = AL.off

    def barrier(skip_pq=False):
        engs = [PE, ACT, DVE, POOL, SQ]
        toks = []
        for E_ in ((PE, ACT, DVE, POOL, SQ) if skip_pq else (PE, ACT, DVE, POOL, PQ, SQ)):
            toks += [t for t in E_.last if t is not None]
        for E_ in engs:
            for t in toks:
                E_.wait(t)

    if S5_ON:
        AL = Alloc(persist_end)
        R_all = AL.get([32, 128], BF16)
        T_all = AL.get([32, 128], BF16)
        O_all = AL.get([32, 128], BF16)
        D_all = AL.get([3, 32, 128], BF16)
        S_bf = AL.get([128], BF16)
        AA = AL.get([64], F32)
        BB = AL.get([64], F32)
        gen_base = AL.off
        H1 = AL.get([32, 512], BF16)
        H2 = AL.get([32, 128], BF16)
        GG = AL.get([2, 128, 32], BF16)
        G2c = AL.get([2, 32, 64], BF16)
        XS = AL.get([2, 64], F32)
        T1 = AL.get([64], F32)
        T2 = AL.get([64], F32)
        sstat = AL.get([2, 8], F32)
        scratch_base = AL.off
        U_own = yb.rearrange("p s j t -> p s (j t)").rearrange("p s (g c) -> p s g c", c=128)
        dram = dict(lam_re=lam_re, lam_im=lam_im, log_dt=log_dt, b_re=b_re, b_im=b_im, c_re=c_re, c_im=c_im,
                    s5_d=s5_d, c_mask_toep=c_mask_toep)
        S_f = Alloc(ARENA_BYTES - 512).get([128], F32)
        run(SQ, lambda e: e.dma_start(out=S_f, in_=c_swap_f[:, :]), writes=[B_const], acc=True)
        run(SQ, lambda e: e.dma_start(out=S_bf, in_=c_swap_bf[:, :]), writes=[B_const], acc=True)
        G_gen = emit_s5_gen(run, Buf, dict(PE=PE, ACT=ACT, DVE=DVE, POOL=POOL, SQ=SQ), Alloc(gen_base), dram,
                            psf, PSF, ident_f, B_const,
                            dict(R_all=R_all, T_all=T_all, O_all=O_all, AA=AA, BB=BB, D_all=D_all, S_f=S_f))
        barrier(skip_pq=True)
        AL = Alloc(scratch_base)
        xcm = AL.get([1, 8, 1024], BF16)
        XCM = [Buf(), Buf()]
        hTs = AL.get([8, 8, 128], BF16)
        HTS = [Buf() for _ in range(8)]
        wBg = AL.get([8, 512], BF16)
        WBG = Buf()
        XBcm = AL.get([32, 8, 16], BF16)
        XBC = [Buf() for _ in range(8)]
        U_tmp = AL.get([32, 128], BF16)
        UT = Buf()
        junk = AL.get([1024], BF16)
        JK = Buf()
        GGB = [Buf(), Buf()]
        G2B = [Buf(), Buf()]
        H2B = [Buf() for _ in range(4)]
        SSB = [Buf(), Buf()]
        SC = Buf()
        run(SQ, lambda e: e.dma_start(out=wBg, in_=win_d[:, :, 2048:2560]), reads=[CONV["winB"]], writes=[WBG])
        for k in range(8):
            run(ACT, lambda e, k=k: e.activation(out=wBg[:, k, :], in_=wBg[:, k, :], func=AF.Copy, scale=g1T[:, k:k + 1]),
                reads=[B_setup], writes=[WBG])
        run(DVE, lambda e: e.memset(XS[:, 0, :], 0.0), writes=[SC])
        cur = 0
        pending_down = []
        for st8 in range(8):
            own = st8 >= 4
            st = st8 % 4
            xsrc = x_own if own else x_prev
            xb_ = 0
            run(PQ, lambda e, xb_=xb_, xsrc=xsrc, st=st: e.dma_start(
                out=xcm[:, xb_], in_=xsrc[st * 1024:(st + 1) * 1024, :].rearrange("(c s) d -> c s d", s=8)),
                writes=[XCM[xb_]])
            ss = sstat[:, xb_, :]
            bulk_conv(1)
            run(ACT, lambda e, ss=ss: e.memzero(ss), writes=[SSB[xb_]])
            for s_ in range(8):
                run(ACT, lambda e, s_=s_, xb_=xb_, ss=ss: e.activation(out=junk, in_=xcm[:, xb_, s_, :], func=AF.Square,
                                                                        accum_out=ss[:, s_:s_ + 1]),
                    reads=[XCM[xb_]], writes=[JK, SSB[xb_]])
            run(ACT, lambda e, ss=ss: e.activation(out=ss, in_=ss, func=AF.Ln, bias=epsc, scale=1.0 / D),
                reads=[B_setup], writes=[SSB[xb_]])
            run(ACT, lambda e, ss=ss: e.activation(out=ss, in_=ss, func=AF.Exp, scale=-0.5), writes=[SSB[xb_]])
            for s_ in range(8):
                pb = s_ % 2
                run(PE, [lambda e, k=k, s_=s_, pb=pb, xb_=xb_: e.transpose(out=psb[:, pb, k * 128:(k + 1) * 128],
                                                                            in_=xcm[:, xb_, s_, k * 128:(k + 1) * 128],
                                                                            identity=ident_bf) for k in range(8)],
                    reads=[XCM[xb_], B_const], writes=[PSB[pb]])
                run(ACT, lambda e, s_=s_, pb=pb: e.activation(out=hTs[:, :, s_, :],
                                                              in_=psb[:, pb, :].rearrange("p (k c) -> p k c", c=128),
                                                              func=AF.Copy),
                    reads=[PSB[pb]], writes=[HTS[s_]])
            for s_ in range(8):
                pb = s_ % 4
                run(PE, [lambda e, k=k, s_=s_, pb=pb: e.matmul(psf[:, pb, :], hTs[:, k, s_, :], wBg[:, k, :],
                                                                start=(k == 0), stop=(k == 7)) for k in range(8)],
                    reads=[HTS[s_], WBG], writes=[PSF[pb]])
                run(ACT, lambda e, s_=s_, pb=pb, ss=ss: e.activation(
                    out=XBcm[:, :, s_, :], in_=psf[:, pb, :].rearrange("p (g h) -> p g h", h=16), func=AF.Copy,
                    scale=ss[:, s_:s_ + 1]),
                    reads=[PSF[pb], SSB[xb_]], writes=[XBC[s_]])
            Ud = U_own[:, st] if own else U_tmp
            UB = YB[st] if own else UT
            for g4 in range(8):
                pb = g4 % 2
                run(PE, [lambda e, gi=gi, g4=g4, pb=pb: e.transpose(
                    out=psb[:, pb, gi * 128:(gi + 1) * 128],
                    in_=XBcm[:, 4 * g4 + gi, :, :].rearrange("p s h -> p (s h)"), identity=ident_bf) for gi in range(4)],
                    reads=XBC + [B_const], writes=[PSB[pb]])
                run(ACT, lambda e, g4=g4, pb=pb, Ud=Ud: e.activation(
                    out=Ud[:, 4 * g4:4 * g4 + 4, :], in_=psb[:, pb, 0:512].rearrange("p (g c) -> p g c", c=128), func=AF.Copy),
                    reads=[PSB[pb]], writes=[UB], acc=(g4 > 0))
            gb_ = st8 % 2
            GGv = GG[:, gb_]
            G2v = G2c[:, gb_]
            for g4 in range(8):
                pb = 4 + g4 % 2
                run(PE, [lambda e, gi=gi, g4=g4, pb=pb, Ud=Ud: e.matmul(
                    psf[:, pb, gi * 128:(gi + 1) * 128], R_all[:, 4 * g4 + gi, :], Ud[:, 4 * g4 + gi, :],
                    start=True, stop=True) for gi in range(4)],
                    reads=[UB, G_gen], writes=[PSF[pb]])
                run(ACT, lambda e, g4=g4, pb=pb, GGv=GGv: e.activation(
                    out=GGv[:, :, 4 * g4:4 * g4 + 4],
                    in_=psf[:, pb, :].rearrange("p (g c) -> p c g", c=128), func=AF.Copy),
                    reads=[PSF[pb]], writes=[GGB[gb_]], acc=(g4 > 0))
            for gh in range(2):
                pb = 4 + gh
                fns = []
                for gl in range(16):
                    g = gh * 16 + gl
                    for i in range(4):
                        lw = ident_bf if i == 3 else D_all[:, 2 - i, g, :]
                        fns.append(lambda e, g=g, gl=gl, i=i, pb=pb, lw=lw, GGv=GGv: e.matmul(
                            psf[:, pb, gl * 32:(gl + 1) * 32], lw,
                            GGv.rearrange("p (C i) g -> p C i g", i=4)[:, :, i, g], start=(i == 0), stop=(i == 3)))
                run(PE, fns, reads=[GGB[gb_], G_gen, B_const], writes=[PSF[pb]])
                run(ACT, lambda e, gh=gh, pb=pb, G2v=G2v: e.activation(
                    out=G2v[:, :, 16 * gh:16 * gh + 16], in_=psf[:, pb, :].rearrange("p (g c) -> p c g", c=32),
                    func=AF.Copy), reads=[PSF[pb]], writes=[G2B[gb_]], acc=(gh > 0))
            for hh in range(2):
                pb = 4 + hh
                run(PE, [lambda e, C=C, hh=hh, pb=pb, G2v=G2v: e.matmul(
                    psf[:, pb, (C % 16) * 32:(C % 16 + 1) * 32], S_bf, G2v[:, C, 0:32], start=True, stop=True)
                    for C in range(16 * hh, 16 * hh + 16)],
                    reads=[G2B[gb_], B_const], writes=[PSF[pb]])
                run(ACT, lambda e, hh=hh, pb=pb, G2v=G2v: e.activation(
                    out=G2v[:, 16 * hh:16 * hh + 16, 32:64], in_=psf[:, pb, :].rearrange("p (c g) -> p c g", g=32),
                    func=AF.Copy), reads=[PSF[pb]], writes=[G2B[gb_]], acc=True)
            if DBG and st8 == 4:
                run(SQ, lambda e, Ud=Ud: e.dma_start(out=dbg_u[:, :], in_=Ud.rearrange("p g c -> p (g c)")), reads=[UB])
            while pending_down:
                pending_down.pop(0)()
            for C in range(32):
                Xc = XS[:, cur, :]
                Xn = XS[:, 1 - cur, :]
                if own:
                    run(DVE, lambda e, Xc=Xc, C=C, st=st: e.tensor_copy(out=H2[:, :, st * 32 + C], in_=Xc[:, 0:32]),
                        reads=[SC], writes=[H2B[st]])
                run(DVE, lambda e, Xc=Xc: e.tensor_tensor(out=T1, in0=AA, in1=Xc, op=ALU.mult), reads=[G_gen], writes=[SC])
                run(DVE, lambda e, Xc=Xc: e.tensor_tensor(out=T2[:, 0:32], in0=BB[:, 0:32], in1=Xc[:, 32:64], op=ALU.mult),
                    writes=[SC])
                run(DVE, lambda e, Xc=Xc: e.tensor_tensor(out=T2[:, 32:64], in0=BB[:, 32:64], in1=Xc[:, 0:32], op=ALU.mult),
                    writes=[SC])
                run(DVE, lambda e: e.tensor_tensor(out=T1, in0=T1, in1=T2, op=ALU.add), writes=[SC])
                run(DVE, lambda e, Xn=Xn, C=C, G2v=G2v: e.tensor_tensor(out=Xn, in0=T1, in1=G2v[:, C, :], op=ALU.add),
                    reads=[G2B[gb_]], writes=[SC])
                cur = 1 - cur
            def emit_down(st=st, GGv=GGv, gb_=gb_):
                for g4 in range(8):
                    pb = g4 % 4
                    fns = []
                    for gi in range(4):
                        g = 4 * g4 + gi
                        hsrc = H2[:, g, st * 32:(st + 1) * 32]
                        gsrc = GGv.rearrange("p (C i) g -> p C i g", i=4)
                        for j in range(4):
                            terms = [(ident_bf if j == 0 else D_all[:, j - 1, g, :], hsrc)]
                            for i in range(j):
                                kpow = j - 1 - i
                                terms.append((ident_bf if kpow == 0 else D_all[:, kpow - 1, g, :], gsrc[:, :, i, g]))
                            for ti, (lw, rh) in enumerate(terms):
                                fns.append(lambda e, gi=gi, j=j, pb=pb, lw=lw, rh=rh, ti=ti, nt=len(terms): e.matmul(
                                    psf[:, pb, gi * 128 + j * 32:gi * 128 + (j + 1) * 32], lw, rh,
                                    start=(ti == 0), stop=(ti == nt - 1)))
                    run(PE, fns, reads=[H2B[st], GGB[gb_], G_gen, B_const], writes=[PSF[pb]])
                    run(ACT, lambda e, g4=g4, pb=pb, st=st: e.activation(
                        out=H1[:, 4 * g4:4 * g4 + 4, st * 128:(st + 1) * 128].rearrange("p g (c j) -> p g j c", j=4),
                        in_=psf[:, pb, :].rearrange("p (g j c) -> p g j c", j=4, c=32), func=AF.Copy),
                        reads=[PSF[pb]], writes=[H1B], acc=True)
            if own:
                pending_down.append(emit_down)
        while pending_down:
            pending_down.pop(0)()
        barrier()
        AL = Alloc(scratch_base)
        Ys2d_ = AL.get([2, 32, 128], BF16)
        YS_ = [Buf(), Buf()]
        Ycm_ = AL.get([2, 8, 512], BF16)
        YC_ = [Buf(), Buf()]
        yg_ = AL.get([2, 4, 1024], BF16)
        YG_ = [Buf(), Buf()]
        wglu = AL.get([4, 512], BF16)
        WGL = Buf()
        sgl = AL.get([2, 512], F32)
        SGL = [Buf(), Buf()]
        run(SQ, lambda e: e.dma_start(out=wglu, in_=wglu_d[:, :, :]), reads=[CONV["wglu"]], writes=[WGL])
        for st in range(4):
            Ys2d, YS, Ycm, YC, yg, YG = Ys2d_[:, st % 2], YS_[st % 2], Ycm_[:, st % 2], YC_[st % 2], yg_[:, st % 2], YG_[st % 2]
            for g4 in range(8):
                pb = g4 % 4
                fns = []
                for gi in range(4):
                    g = 4 * g4 + gi
                    fns.append(lambda e, Ys2d=Ys2d, Ycm=Ycm, yg=yg, g=g, gi=gi, pb=pb, st=st: e.matmul(psf[:, pb, gi * 128:(gi + 1) * 128], T_all[:, g, :],
                                                                     U_own[:, st, g, :], start=True, stop=False))
                    fns.append(lambda e, Ys2d=Ys2d, Ycm=Ycm, yg=yg, g=g, gi=gi, pb=pb, st=st: e.matmul(psf[:, pb, gi * 128:(gi + 1) * 128], O_all[:, g, :],
                                                                     H1[:, g, st * 128:(st + 1) * 128], start=False, stop=True))
                run(PE, fns, reads=[YB[st], H1B, G_gen], writes=[PSF[pb]])
                run(DVE, lambda e, Ys2d=Ys2d, Ycm=Ycm, yg=yg, g4=g4, pb=pb: e.tensor_copy(out=Ys2d[:, 4 * g4:4 * g4 + 4, :],
                                                               in_=psf[:, pb, :].rearrange("p (g c) -> p g c", c=128)),
                    reads=[PSF[pb]], writes=[YS], acc=(g4 > 0))
            for g4 in range(8):
                pb = g4 % 2
                run(PE, [lambda e, Ys2d=Ys2d, Ycm=Ycm, yg=yg, gi=gi, g4=g4, pb=pb: e.transpose(out=psb[:, pb, gi * 128:(gi + 1) * 128],
                                                                    in_=Ys2d[:, 4 * g4 + gi, :], identity=ident_bf)
                         for gi in range(4)], reads=[YS, B_const], writes=[PSB[pb]])
                run(DVE, lambda e, Ys2d=Ys2d, Ycm=Ycm, yg=yg, g4=g4, pb=pb: e.tensor_copy(
                    out=Ycm[:, :, 64 * g4:64 * g4 + 64].rearrange("p t (g h) -> p t g h", h=16),
                    in_=psb[:, pb, 0:512].rearrange("p (g t h) -> p t g h", t=8, h=16)),
                    reads=[PSB[pb]], writes=[YC], acc=(g4 > 0))
            for tau in range(8):
                pb = tau % 2
                run(PE, [lambda e, Ys2d=Ys2d, Ycm=Ycm, yg=yg, j=j, tau=tau, pb=pb: e.transpose(out=psb[:, pb, j * 128:(j + 1) * 128],
                                                                    in_=Ycm[:, tau, j * 128:(j + 1) * 128], identity=ident_bf)
                         for j in range(4)], reads=[YC, B_const], writes=[PSB[pb]])
                run(ACT, lambda e, Ys2d=Ys2d, Ycm=Ycm, yg=yg, tau=tau, pb=pb: e.activation(
                    out=yg.rearrange("p j (c s) -> p j c s", s=8)[:, :, :, tau],
                    in_=psb[:, pb, 0:512].rearrange("p (j c) -> p j c", c=128), func=AF.Gelu_apprx_tanh),
                    reads=[PSB[pb]], writes=[YG], acc=(tau > 0))
            for j2 in range(4):
                for hf in range(2):
                    pb = 4 + hf
                    run(PE, [lambda e, Ys2d=Ys2d, Ycm=Ycm, yg=yg, j=j, j2=j2, hf=hf, pb=pb: e.matmul(psf[:, pb, :], wglu[:, j, j2 * 128:(j2 + 1) * 128],
                                                                          yg[:, j, hf * 512:(hf + 1) * 512],
                                                                          start=(j == 0), stop=(j == 3)) for j in range(4)],
                        reads=[YG, WGL], writes=[PSF[pb]])
                    run(ACT, lambda e, Ys2d=Ys2d, Ycm=Ycm, yg=yg, j2=j2, hf=hf, pb=pb: e.activation(out=sgl[:, hf, :], in_=psf[:, pb, :], func=AF.Sigmoid,
                                                                          bias=bgluT[:, j2:j2 + 1], scale=1.0),
                        reads=[PSF[pb], B_setup], writes=[SGL[hf]])
                    run(DVE, lambda e, Ys2d=Ys2d, Ycm=Ycm, yg=yg, j2=j2, hf=hf, st=st: e.tensor_tensor(out=yb[:, st, j2, hf * 512:(hf + 1) * 512],
                                                                            in0=yg[:, j2, hf * 512:(hf + 1) * 512],
                                                                            in1=sgl[:, hf, :], op=ALU.mult),
                        reads=[YG, SGL[hf]], writes=[YB[st]], acc=not (j2 == 0 and hf == 0))
        barrier()
        if DBG:
            run(SQ, lambda e: e.dma_start(out=dbg_yb[:, :], in_=yb.rearrange("p s j t -> p (s j t)")), writes=[Buf()])
            run(SQ, lambda e: e.dma_start(out=dbg_h1[:, :], in_=H1.rearrange("p g c -> p (g c)")), writes=[Buf()])
            run(SQ, lambda e: e.dma_start(out=dbg_ys[:, :], in_=Ys2d.rearrange("p g c -> p (g c)")), writes=[Buf()])
            run(SQ, lambda e: e.dma_start(out=dbg_ycm[:, :], in_=Ycm.rearrange("p t c -> p (t c)")), writes=[Buf()])
            run(SQ, lambda e: e.dma_start(out=dbg_yg[:, :], in_=yg.rearrange("p j t -> p (j t)")), writes=[Buf()])
            barrier()
    else:
        bulk_conv()
        for st in range(4):
            run(POOL, lambda e, st=st: e.memset(yb[:, st], 0.0), writes=[YB[st]])

    bulk_conv()
    AL = Alloc(persist_end)
    xb2 = AL.get([2, 4, D], F32)
    X2 = [[Buf() for _ in range(4)] for _ in range(2)]
    ostage = AL.get([1, D], F32)
    OST = [Buf()]
    hTa = AL.get([8, 512], BF16)
    HTa = [Buf() for _ in range(4)]
    hTb = AL.get([8, 512], BF16)
    HTb = [Buf() for _ in range(4)]
    hb = AL.get([2, D], BF16)
    HB = [Buf(), Buf()]
    vg = AL.get([1, D], F32)
    VG = [Buf()]
    nb_ = AL.get([2, D], BF16)
    NB = [Buf() for _ in range(2)]
    mx = AL.get([8, 512], BF16)
    MX = [Buf() for _ in range(4)]
    scr = AL.get([2, 512], F32)
    SCR = [Buf() for _ in range(2)]
    ya = AL.get([8, 512], BF16)
    YA = [Buf() for _ in range(8)]
    gts = AL.get([2, 512], F32)
    GT = [Buf() for _ in range(2)]
    mg = AL.get([8, 512], BF16)
    MG = [Buf() for _ in range(8)]
    act_off = AL.off
    act = AL.get([NF, 512], BF16)
    AC = [Buf() for _ in range(NF)]
    gfb = AL.get([D], F32)
    biasT = AL.get([8, 128], BF16)
    wsT = AL.get([8, 128], BF16)
    stats = AL.get([64], F32)
    AL2 = Alloc(act_off)
    wstage = AL2.get([8, 128], F32)
    bsb = AL2.get([8, 128], F32)
    maskws = AL2.get([128], F32)
    ones_bf = AL2.get([128], BF16)
    NRING = 6
    ring = AL.get([NRING, 4096], BF16)
    RING = [Buf() for _ in range(NRING)]
    STATB = [Buf() for _ in range(4)]
    STATF = [Buf() for _ in range(4)]
    pending_final = []

    run(SQ, lambda e: e.dma_start(out=gfb, in_=norm_f_g.partition_broadcast(128)), writes=[B_setup], acc=True)
    run(SQ, lambda e: e.dma_start(out=bsb.rearrange("p g i -> p (g i)"),
                                  in_=gmlp_bs.rearrange("g i -> (g i)").partition_broadcast(128)),
        writes=[B_setup], acc=True)
    run(SQ, lambda e: e.dma_start(out=wstage, in_=gmlp_ws.rearrange("g i j -> i g j")),
        writes=[B_setup], acc=True)
    run(SQ, lambda e: e.dma_start(out=maskws, in_=c_mask_ws[:, :]), writes=[B_setup], acc=True)
    for g in range(8):
        pb = g % 6
        run(PE, lambda e, g=g, pb=pb: e.transpose(out=psf[:, pb, 0:128], in_=wstage[:, g, :], identity=ident_f),
            reads=[B_setup, B_const], writes=[PSF[pb]])
        run(DVE, lambda e, g=g, pb=pb: e.tensor_tensor(out=wsT[:, g, :], in0=psf[:, pb, 0:128], in1=maskws, op=ALU.mult),
            reads=[PSF[pb], B_setup], writes=[B_setup], acc=True)
    run(POOL, lambda e: e.memset(ones_bf, 1.0), writes=[B_setup], acc=True)
    for h2 in range(2):
        run(PE, lambda e, h2=h2: e.matmul(psf[:, h2, :], ones_bf, wsT[:, 4 * h2:4 * h2 + 4, :].rearrange("p g i -> p (g i)"),
                                          start=True, stop=True),
            reads=[B_setup], writes=[PSF[h2]])
        for gg in range(4):
            g = 4 * h2 + gg
            run(DVE, lambda e, g=g, gg=gg, h2=h2: e.scalar_tensor_tensor(
                out=biasT[:, g, :], in0=psf[:, h2, gg * 128:(gg + 1) * 128], scalar=lnbT[:, g:g + 1],
                in1=bsb[:, g, :], op0=ALU.mult, op1=ALU.add),
                reads=[PSF[h2], B_setup], writes=[B_setup], acc=True)

    for f in range(NF):
        AC[f].r = list(B_setup.w) + list(B_setup.r)

    ring_i = [0]

    def wload(src_ap, shape3, conv_key):
        i = ring_i[0]
        ring_i[0] = (i + 1) % NRING
        a, b = shape3
        dst = ring[:, i, 0:a * b].rearrange("p (a b) -> p a b", b=b)
        run(SQ, lambda e: e.dma_start(out=dst, in_=src_ap), reads=[CONV[conv_key]], writes=[RING[i]])
        return dst, RING[i]

    psf_i = [0]

    def next_psf():
        i = psf_i[0]
        psf_i[0] = (i + 1) % 6
        return i

    scr_i = [0]

    def next_scr():
        i = scr_i[0]
        scr_i[0] = (i + 1) % 2
        return i

    def rms_scale(b, tagbase, xs):
        s = b % 2
        ssq = stats[:, tagbase + b:tagbase + b + 1]
        run(DVE, lambda e, ssq=ssq: e.memset(ssq, 0.0), writes=[STATB[b]])
        run(ACT, lambda e, b=b, s=s, ssq=ssq, xs=xs: e.activation(out=hb[:, s, :], in_=xb2[:, xs, b, :], func=AF.Square,
                                                                   accum_out=ssq),
            reads=[X2[xs][b]], writes=[HB[s], STATB[b]])
        run(ACT, lambda e, ssq=ssq: e.activation(out=ssq, in_=ssq, func=AF.Sqrt, bias=epsc, scale=1.0 / D),
            reads=[B_setup], writes=[STATB[b]])
        run(DVE, lambda e, ssq=ssq: e.reciprocal(out=ssq, in_=ssq), writes=[STATB[b]])
        run(DVE, lambda e, b=b, s=s, ssq=ssq, xs=xs: e.tensor_scalar(out=hb[:, s, :], in0=xb2[:, xs, b, :], scalar1=ssq,
                                                                      scalar2=None, op0=ALU.mult),
            reads=[X2[xs][b], STATB[b]], writes=[HB[s]])

    def rms_transpose(b, gT, hTd, HTd):
        s = b % 2
        pb = b % 2
        run(PE, [lambda e, k=k, s=s, pb=pb: e.transpose(out=psb[:, pb, k * 128:(k + 1) * 128],
                                                        in_=hb[:, s, k * 128:(k + 1) * 128], identity=ident_bf)
                 for k in range(8)],
            reads=[HB[s], B_const], writes=[PSB[pb]])
        run(DVE, lambda e, b=b, pb=pb, hTd=hTd, gT=gT: e.tensor_tensor(
            out=hTd[:, :, b * 128:(b + 1) * 128], in0=psb[:, pb, :].rearrange("p (k t) -> p k t", t=128),
            in1=gT.unsqueeze(2).to_broadcast([128, 8, 128]), op=ALU.mult),
            reads=[PSB[pb], B_setup], writes=[HTd[b]])

    def rms_to_hT(gT, tagbase, xs, hTd, HTd):
        rms_scale(0, tagbase, xs)
        rms_scale(1, tagbase, xs)
        rms_transpose(0, gT, hTd, HTd)
        rms_scale(2, tagbase, xs)
        rms_transpose(1, gT, hTd, HTd)
        rms_scale(3, tagbase, xs)
        rms_transpose(2, gT, hTd, HTd)
        rms_transpose(3, gT, hTd, HTd)

    def fm_to_x(src, SRC, xs, blocks=(0, 1, 2, 3)):
        for b in blocks:
            pb = b % 2
            run(PE, [lambda e, m=m, b=b, pb=pb: e.transpose(out=psb[:, pb, m * 128:(m + 1) * 128],
                                                            in_=src[:, m, b * 128:(b + 1) * 128], identity=ident_bf)
                     for m in range(8)],
                reads=list(SRC) + [B_const], writes=[PSB[pb]])
            run(DVE, lambda e, b=b, pb=pb, xs=xs: e.tensor_tensor(out=xb2[:, xs, b, :], in0=xb2[:, xs, b, :], in1=psb[:, pb, :],
                                                                   op=ALU.add),
                reads=[PSB[pb]], writes=[X2[xs][b]])

    def load_x(tt):
        for b in range(4):
            run(PQ, lambda e, b=b, tt=tt: e.dma_start(out=xb2[:, tt % 2, b, :],
                                                      in_=x_own[tt * 512 + b * 128:tt * 512 + (b + 1) * 128, :]),
                writes=[X2[tt % 2][b]])

    load_x(0)
    rms_to_hT(g1T, 40, 0, hTa, HTa)

    for tt in range(NTILES):
        t0 = tt * 512
        xs = tt % 2
        st_, so_ = tt // 2, (tt % 2) * 512
        wv0, WV0 = wload(win_d[:, :, 1024:1536], (8, 512), "win")
        wv1, WV1 = wload(win_d[:, :, 1536:2048], (8, 512), "win")
        fb = [0]

        def fbank():
            fb[0] ^= 1
            return 4 + fb[0]

        def v_block(b):
            s = 0
            for hf, (wv, WV) in enumerate(((wv0, WV0), (wv1, WV1))):
                run(PE, [lambda e, k=k, b=b, hf=hf, wv=wv: e.matmul(psf[:, hf, :], hTa[:, k, b * 128:(b + 1) * 128],
                                                                     wv[:, k, :], start=(k == 0), stop=(k == 7))
                         for k in range(8)],
                    reads=[HTa[b], WV], writes=[PSF[hf]])
            s1 = stats[:, 8 + 2 * b:9 + 2 * b]
            s2 = stats[:, 9 + 2 * b:10 + 2 * b]
            mean = stats[:, 16 + 2 * b:17 + 2 * b]
            rstd = stats[:, 17 + 2 * b:18 + 2 * b]
            run(DVE, lambda e, b=b: e.memset(stats[:, 8 + 2 * b:10 + 2 * b], 0.0), writes=[STATB[b]])
            run(ACT, lambda e, s=s, s1=s1: e.activation(out=vg[:, s, :], in_=psf[:, 0:2, :].rearrange("p a b -> p (a b)"),
                                                         func=AF.Gelu_apprx_tanh, accum_out=s1),
                reads=[PSF[0], PSF[1]], writes=[VG[s], STATB[b]])
            sn = b % 2
            run(ACT, lambda e, s=s, sn=sn, s2=s2: e.activation(out=nb_[:, sn, :], in_=vg[:, s, :], func=AF.Square, accum_out=s2),
                reads=[VG[s]], writes=[NB[sn], STATB[b]])
            run(DVE, lambda e, mean=mean, s1=s1: e.tensor_scalar(out=mean, in0=s1, scalar1=1.0 / D, scalar2=None, op0=ALU.mult),
                writes=[STATB[b]])
            run(DVE, lambda e, mean=mean, rstd=rstd: e.tensor_tensor(out=rstd, in0=mean, in1=mean, op=ALU.mult),
                writes=[STATB[b]])
            run(DVE, lambda e, s2=s2, rstd=rstd: e.scalar_tensor_tensor(out=rstd, in0=s2, scalar=1.0 / D, in1=rstd,
                                                                         op0=ALU.mult, op1=ALU.subtract),
                writes=[STATB[b]])
            run(ACT, lambda e, rstd=rstd: e.activation(out=rstd, in_=rstd, func=AF.Sqrt, bias=epsc, scale=1.0),
                reads=[B_setup], writes=[STATB[b]])
            run(DVE, lambda e, rstd=rstd: e.reciprocal(out=rstd, in_=rstd), writes=[STATB[b]])
            run(DVE, lambda e, s=s, sn=sn, mean=mean, rstd=rstd: e.tensor_scalar(out=nb_[:, sn, :], in0=vg[:, s, :], scalar1=mean,
                                                                                  scalar2=rstd, op0=ALU.subtract, op1=ALU.mult),
                reads=[VG[s], STATB[b]], writes=[NB[sn]])

        def sp_block(b):
            s = b % 2
            run(PE, [lambda e, g=g, s=s: e.matmul(psf[:, 2 + g // 4, (g % 4) * 128:(g % 4 + 1) * 128],
                                                  nb_[:, s, g * 128:(g + 1) * 128], wsT[:, g, :], start=True, stop=True)
                     for g in range(8)],
                reads=[NB[s], B_setup], writes=[PSF[2], PSF[3]])
            run(ACT, lambda e, b=b: e.activation(out=mx[:, :, b * 128:(b + 1) * 128],
                                                 in_=psf[:, 2:4, :].rearrange("p a (g i) -> p (a g) i", i=128),
                                                 func=AF.Copy),
                reads=[PSF[2], PSF[3]], writes=[MX[b]])
            scv = scr.rearrange("p a (b i) -> p (a b) i", i=128)
            bs_ = slice(b * 128, (b + 1) * 128)
            run(DVE, lambda e, bs_=bs_: e.tensor_tensor(out=scv, in0=mx[:, :, bs_],
                                                        in1=lngT.unsqueeze(2).to_broadcast([128, 8, 128]), op=ALU.mult),
                reads=[MX[b], B_setup], writes=[SCR[0], SCR[1]])
            run(DVE, lambda e: e.tensor_tensor(out=scv, in0=scv, in1=biasT, op=ALU.add), reads=[B_setup], writes=[SCR[0], SCR[1]])
            run(DVE, lambda e, bs_=bs_: e.tensor_tensor(out=ya[:, :, bs_], in0=ya[:, :, bs_], in1=scv, op=ALU.mult),
                reads=[SCR[0], SCR[1]], writes=YA)

        def u_half(half):
            wu, WU = wload(win_d[:, :, half * 512:(half + 1) * 512], (8, 512), "win")
            for gg in range(4):
                g = half * 4 + gg
                pb = fbank()
                run(PE, [lambda e, k=k, gg=gg, pb=pb, wu=wu: e.matmul(psf[:, pb, :], wu[:, k, gg * 128:(gg + 1) * 128],
                                                                       hTa[:, k, :], start=(k == 0), stop=(k == 7))
                         for k in range(8)],
                    reads=HTa + [WU], writes=[PSF[pb]])
                run(ACT, lambda e, pb=pb, g=g: e.activation(out=ya[:, g, :], in_=psf[:, pb, :], func=AF.Gelu_apprx_tanh),
                    reads=[PSF[pb]], writes=[YA[g]])

        def gate_half(col0, half, slot0):
            wg, WG = wload(win_d[:, :, col0 + half * 512:col0 + (half + 1) * 512], (8, 512), "win")
            for mm in range(4):
                m = half * 4 + mm
                pb = fbank()
                run(PE, [lambda e, k=k, mm=mm, pb=pb, wg=wg: e.matmul(psf[:, pb, :], wg[:, k, mm * 128:(mm + 1) * 128],
                                                                       hTa[:, k, :], start=(k == 0), stop=(k == 7))
                         for k in range(8)], reads=HTa + [WG], writes=[PSF[pb]])
                run(ACT, lambda e, pb=pb, m=m, slot0=slot0: e.activation(out=act[:, slot0 + m, :], in_=psf[:, pb, :],
                                                                          func=AF.Tanh, scale=0.5),
                    reads=[PSF[pb]], writes=[AC[slot0 + m]])

        v_block(0)
        u_half(0)
        v_block(1)
        u_half(1)
        while pending_final:
            pending_final.pop(0)()
        if tt + 1 < NTILES:
            load_x(tt + 1)
        sp_block(0)
        v_block(2)
        gate_half(2560, 0, 0)
        sp_block(1)
        v_block(3)
        gate_half(3584, 0, 8)
        sp_block(2)
        gate_half(2560, 1, 0)
        gate_half(3584, 1, 8)
        sp_block(3)
        for half in range(2):
            wbb, WBB = wload(wb_d[:, :, half * 512:(half + 1) * 512], (4, 512), "wb")
            wa_, WA_ = wload(wa_d[:, :, half * 512:(half + 1) * 512], (8, 512), "wa")
            for mm in range(4):
                m = half * 4 + mm
                p3, p4 = next_psf(), next_psf()
                run(PE, [lambda e, k=k, mm=mm, p3=p3, wa_=wa_: e.matmul(psf[:, p3, :], wa_[:, k, mm * 128:(mm + 1) * 128],
                                                                         ya[:, k, :], start=(k == 0), stop=(k == 7))
                         for k in range(8)], reads=YA + [WA_], writes=[PSF[p3]])
                run(DVE, lambda e, p3=p3, m=m: e.scalar_tensor_tensor(out=gts[:, 0, :], in0=act[:, m, :], scalar=1.0,
                                                                       in1=psf[:, p3, :], op0=ALU.add, op1=ALU.mult),
                    reads=[AC[m], PSF[p3]], writes=[GT[0]])
                run(PE, [lambda e, j=j, mm=mm, p4=p4, wbb=wbb, st_=st_, so_=so_: e.matmul(
                    psf[:, p4, :], wbb[:, j, mm * 128:(mm + 1) * 128], yb[:, st_, j, so_:so_ + 512],
                    start=(j == 0), stop=(j == 3)) for j in range(4)], reads=[YB[st_], WBB], writes=[PSF[p4]])
                run(DVE, lambda e, p4=p4, m=m: e.scalar_tensor_tensor(out=gts[:, 1, :], in0=act[:, 8 + m, :], scalar=1.0,
                                                                       in1=psf[:, p4, :], op0=ALU.add, op1=ALU.mult),
                    reads=[AC[8 + m], PSF[p4]], writes=[GT[1]])
                run(DVE, lambda e, m=m: e.tensor_tensor(out=mg[:, m, :], in0=gts[:, 0, :], in1=gts[:, 1, :], op=ALU.add),
                    reads=[GT[0], GT[1]], writes=[MG[m]])
        for half in range(2):
            wo_, WO_ = wload(wo_d[:, :, half * 512:(half + 1) * 512], (8, 512), "wo")
            for mm in range(4):
                m = half * 4 + mm
                pb = next_psf()
                run(PE, [lambda e, k=k, mm=mm, pb=pb, wo_=wo_: e.matmul(psf[:, pb, :], wo_[:, k, mm * 128:(mm + 1) * 128],
                                                                         mg[:, k, :], start=(k == 0), stop=(k == 7))
                         for k in range(8)], reads=MG + [WO_], writes=[PSF[pb]])
                run(ACT, lambda e, m=m, pb=pb: e.activation(out=ya[:, m, :], in_=psf[:, pb, :], func=AF.Copy, scale=0.5),
                    reads=[PSF[pb]], writes=[YA[m]])
        fm_to_x(ya, YA, xs, (0, 1, 2, 3))
        rms_scale(0, 4, xs)
        rms_scale(1, 4, xs)
        rms_transpose(0, g2T, hTb, HTb)
        rms_scale(2, 4, xs)
        rms_transpose(1, g2T, hTb, HTb)
        rms_scale(3, 4, xs)
        rms_transpose(2, g2T, hTb, HTb)
        rms_transpose(3, g2T, hTb, HTb)
        for fc in range(6):
            nfc = 4 if fc < 5 else 2
            if tt + 1 < NTILES:
                tg = 44 + 4 * ((tt + 1) % 2)
                if fc == 1:
                    rms_scale(0, tg, 1 - xs)
                if 2 <= fc <= 4:
                    rms_scale(fc - 1, tg, 1 - xs)
                    rms_transpose(fc - 2, g1T, hTa, HTa)
                if fc == 5:
                    rms_transpose(3, g1T, hTa, HTa)
            wg_, WG_ = wload(wfg_d[:, :, fc * 512:fc * 512 + nfc * 128], (8, nfc * 128), "wfg")
            wu_, WU_ = wload(wfu_d[:, :, fc * 512:fc * 512 + nfc * 128], (8, nfc * 128), "wfu")
            for ff in range(nfc):
                f = fc * 4 + ff
                p1, p2 = next_psf(), next_psf()
                run(PE, [lambda e, k=k, ff=ff, p1=p1, wg_=wg_: e.matmul(psf[:, p1, :], wg_[:, k, ff * 128:(ff + 1) * 128],
                                                                         hTb[:, k, :], start=(k == 0), stop=(k == 7))
                         for k in range(8)], reads=HTb + [WG_], writes=[PSF[p1]])
                sa = next_scr()
                run(ACT, lambda e, p1=p1, sa=sa: e.activation(out=scr[:, sa, :], in_=psf[:, p1, :], func=AF.Silu),
                    reads=[PSF[p1]], writes=[SCR[sa]])
                run(PE, [lambda e, k=k, ff=ff, p2=p2, wu_=wu_: e.matmul(psf[:, p2, :], wu_[:, k, ff * 128:(ff + 1) * 128],
                                                                         hTb[:, k, :], start=(k == 0), stop=(k == 7))
                         for k in range(8)], reads=HTb + [WU_], writes=[PSF[p2]])
                run(DVE, lambda e, f=f, p2=p2, sa=sa: e.tensor_tensor(out=act[:, f, :], in0=scr[:, sa, :], in1=psf[:, p2, :],
                                                                       op=ALU.mult),
                    reads=[SCR[sa], PSF[p2]], writes=[AC[f]])
        for m in range(8):
            wd_, WD_ = wload(wfd_d[:, m, :, :], (NF, 128), "wfd")
            pb = next_psf()
            run(PE, [lambda e, f=f, pb=pb, wd_=wd_: e.matmul(psf[:, pb, :], wd_[:, f, :], act[:, f, :],
                                                              start=(f == 0), stop=(f == NF - 1))
                     for f in range(NF)], reads=AC + [WD_], writes=[PSF[pb]])
            run(ACT, lambda e, m=m, pb=pb: e.activation(out=ya[:, m, :], in_=psf[:, pb, :], func=AF.Copy),
                reads=[PSF[pb]], writes=[YA[m]])
        fm_to_x(ya, YA, xs)
        def final_norm(xs=xs, t0=t0):
            scr_flat = scr.rearrange("p a b -> p (a b)")
            for b in range(4):
                ssq = stats[:, 32 + b:33 + b]
                if b % 2 == 0:
                    odst, OB = ostage[:, 0, :], [OST[0]]
                else:
                    odst, OB = scr_flat, [SCR[0], SCR[1]]
                hs = b % 2
                run(DVE, lambda e, ssq=ssq: e.memset(ssq, 0.0), writes=[STATF[b]])
                run(ACT, lambda e, b=b, hs=hs, ssq=ssq, xs=xs: e.activation(out=hb[:, hs, :], in_=xb2[:, xs, b, :], func=AF.Square,
                                                                             accum_out=ssq),
                    reads=[X2[xs][b]], writes=[HB[hs], STATF[b]])
                run(ACT, lambda e, ssq=ssq: e.activation(out=ssq, in_=ssq, func=AF.Sqrt, bias=epsc, scale=1.0 / D),
                    reads=[B_setup], writes=[STATF[b]])
                run(DVE, lambda e, ssq=ssq: e.reciprocal(out=ssq, in_=ssq), writes=[STATF[b]])
                run(DVE, lambda e, b=b, odst=odst, ssq=ssq, xs=xs: e.scalar_tensor_tensor(out=odst, in0=xb2[:, xs, b, :], scalar=ssq,
                                                                                    in1=gfb, op0=ALU.mult, op1=ALU.mult),
                    reads=[X2[xs][b], STATF[b], B_setup], writes=OB)
                B_out.w.append(run(PQ, lambda e, b=b, odst=odst, t0=t0: e.dma_start(out=out[t0 + b * 128:t0 + (b + 1) * 128, :],
                                                                          in_=odst), reads=OB))
        pending_final.append(final_norm)

    while pending_final:
        pending_final.pop(0)()
    for t in B_out.w:
        POOL.wait(t)

    sem_cms = {n: nc.semaphore(n) for n in all_sem_names}
    sems = {n: cm.__enter__() for n, cm in sem_cms.items()}
    with nc.Block() as block:
        def replay(E):
            def f(eng):
                for o in E.ops:
                    if o[0] == "w":
                        eng.wait_ge(sems[o[1]], o[2])
                    else:
                        ins = o[1](eng)
                        if o[2] is not None:
                            ins.then_inc(sems[o[2]], o[3])
            return f
        block.tensor(replay(PE))
        block.scalar(replay(ACT))
        block.vector(replay(DVE))
        block.gpsimd(replay(POOL))
        block.sync(replay(SQ))
    for cm in sem_cms.values():
        cm.__exit__(None, None, None)
    psb_cm.__exit__(None, None, None)
    psf_cm.__exit__(None, None, None)
    arena_cm.__exit__(None, None, None)
    return nc


B_out = Buf()


def emit_s5_gen(run, Buf, E, AL, dram, psf, PSF, ident_f, B_const, outs):
    PE, ACT, DVE, POOL, SQ = E["PE"], E["ACT"], E["DVE"], E["POOL"], E["SQ"]
    G = Buf()
    GT_ = Buf()

    def ld(fn):
        run(SQ, fn, writes=[G], acc=True)

    def dv(fn):
        run(DVE, fn, writes=[G])

    def ac(fn):
        run(ACT, fn, writes=[G])

    lr2 = AL.get([32], F32)
    li2 = AL.get([32], F32)
    ldt = AL.get([32], F32)
    Bre2 = AL.get([32, 16], F32)
    Bim2 = AL.get([32, 16], F32)
    cst_re = AL.get([4, 128], F32)
    cst_im = AL.get([4, 128], F32)
    Cre2 = AL.get([32, 16], F32)
    Cim2 = AL.get([32, 16], F32)
    dcol = AL.get([32], F32)
    maskT = AL.get([128], F32)
    cpi = AL.get([2], F32)
    stg = AL.get([3, 128], F32)
    for h in range(2):
        ps = slice(64 * h, 64 * h + 64)
        ld(lambda e, ps=ps: e.dma_start(out=stg[0:32, 0, ps], in_=dram["lam_re"][:, :]))
        ld(lambda e, ps=ps: e.dma_start(out=stg[0:32, 1, ps], in_=dram["lam_im"][:, :]))
        ld(lambda e, ps=ps: e.dma_start(out=Bre2[ps, :, :], in_=dram["b_re"].rearrange("g p h -> p g h")))
        ld(lambda e, ps=ps: e.dma_start(out=Bim2[ps, :, :], in_=dram["b_im"].rearrange("g p h -> p g h")))
        cs = slice(64 * h, 64 * h + 64)
        ld(lambda e, cs=cs: e.dma_start(out=cst_re[:, :, cs], in_=dram["c_re"].rearrange("(r a) h p -> (a h) r p", a=8)))
        ld(lambda e, cs=cs: e.dma_start(out=cst_im[:, :, cs], in_=dram["c_im"].rearrange("(r a) h p -> (a h) r p", a=8)))
    ld(lambda e: e.dma_start(out=ldt, in_=dram["log_dt"].partition_broadcast(128)))
    for s in range(8):
        ld(lambda e, s=s: e.dma_start(out=stg[0:32, 2, 16 * s:16 * s + 16], in_=dram["s5_d"][:, :]))
    for i_, dst_ in enumerate((lr2, li2, dcol)):
        run(PE, lambda e, i_=i_: e.transpose(out=psf[:, 1, 0:32], in_=stg[0:32, i_, :], identity=ident_f[0:32, 0:32]),
            reads=[G, B_const], writes=[PSF[1]])
        run(DVE, lambda e, dst_=dst_: e.tensor_copy(out=dst_, in_=psf[:, 1, 0:32]), reads=[PSF[1]], writes=[G])
    ld(lambda e: e.dma_start(out=maskT, in_=dram["c_mask_toep"][:, :]))
    run(POOL, lambda e: e.memset(cpi[:, 0:1], -float(np.pi)), writes=[G], acc=True)
    run(POOL, lambda e: e.memset(cpi[:, 1:2], float(np.pi) / 2.0), writes=[G], acc=True)

    for (cst, C2) in ((cst_re, Cre2), (cst_im, Cim2)):
        for r in range(4):
            run(PE, lambda e, cst=cst, r=r: e.transpose(out=psf[:, 0, 0:128], in_=cst[:, r, :], identity=ident_f),
                reads=[G, B_const], writes=[PSF[0]])
            run(DVE, lambda e, C2=C2, r=r: e.tensor_copy(out=C2[:, r * 8:(r + 1) * 8, :],
                                                         in_=psf[:, 0, 0:128].rearrange("p (a h) -> p a h", h=16)),
                reads=[PSF[0]], writes=[G])

    def t32():
        return AL.get([32], F32)

    dt, lrd, lid, ar, ai = (t32() for _ in range(5))
    yy = t32()
    dv(lambda e: e.tensor_scalar(out=yy, in0=ldt, scalar1=0.125, scalar2=None, op0=ALU.mult))
    dv(lambda e: e.tensor_scalar(out=dt, in0=yy, scalar1=1.0 / 12.0, scalar2=1.0, op0=ALU.mult, op1=ALU.add))
    for kk in range(11, 0, -1):
        dv(lambda e: e.tensor_tensor(out=dt, in0=dt, in1=yy, op=ALU.mult))
        dv(lambda e, kk=kk: e.tensor_scalar(out=dt, in0=dt, scalar1=1.0 / kk, scalar2=1.0, op0=ALU.mult, op1=ALU.add))
    for _ in range(3):
        dv(lambda e: e.tensor_tensor(out=dt, in0=dt, in1=dt, op=ALU.mult))
    dv(lambda e: e.tensor_tensor(out=lrd, in0=lr2, in1=dt, op=ALU.mult))
    dv(lambda e: e.tensor_tensor(out=lid, in0=li2, in1=dt, op=ALU.mult))
    wr, wi, pr_, pi2, qr, qi, ur, ui, e1, e2, e3 = (t32() for _ in range(11))
    dv(lambda e: e.tensor_scalar(out=wr, in0=lrd, scalar1=1.0 / 256.0, scalar2=None, op0=ALU.mult))
    dv(lambda e: e.tensor_scalar(out=wi, in0=lid, scalar1=1.0 / 256.0, scalar2=None, op0=ALU.mult))
    dv(lambda e: e.tensor_scalar(out=pr_, in0=wr, scalar1=1.0 / 5.0, scalar2=1.0, op0=ALU.mult, op1=ALU.add))
    dv(lambda e: e.tensor_scalar(out=pi2, in0=wi, scalar1=1.0 / 5.0, scalar2=None, op0=ALU.mult))

    def cm(orr, oii, xr, xi, yr, yi):
        dv(lambda e: e.tensor_tensor(out=e1, in0=xr, in1=yr, op=ALU.mult))
        dv(lambda e: e.tensor_tensor(out=e2, in0=xi, in1=yi, op=ALU.mult))
        dv(lambda e: e.tensor_tensor(out=e3, in0=xr, in1=yi, op=ALU.mult))
        dv(lambda e: e.tensor_tensor(out=oii, in0=xi, in1=yr, op=ALU.mult))
        dv(lambda e: e.tensor_tensor(out=oii, in0=oii, in1=e3, op=ALU.add))
        dv(lambda e: e.tensor_tensor(out=orr, in0=e1, in1=e2, op=ALU.subtract))

    for dv_ in (4.0, 3.0, 2.0):
        cm(qr, qi, wr, wi, pr_, pi2)
        dv(lambda e, dv_=dv_: e.tensor_scalar(out=pr_, in0=qr, scalar1=1.0 / dv_, scalar2=1.0, op0=ALU.mult, op1=ALU.add))
        dv(lambda e, dv_=dv_: e.tensor_scalar(out=pi2, in0=qi, scalar1=1.0 / dv_, scalar2=None, op0=ALU.mult))
    cm(ur, ui, wr, wi, pr_, pi2)

    def sq_u():
        dv(lambda e: e.tensor_tensor(out=e1, in0=ur, in1=ur, op=ALU.mult))
        dv(lambda e: e.tensor_tensor(out=e2, in0=ui, in1=ui, op=ALU.mult))
        dv(lambda e: e.tensor_tensor(out=e1, in0=e1, in1=e2, op=ALU.subtract))
        dv(lambda e: e.tensor_scalar(out=e3, in0=ur, scalar1=1.0, scalar2=None, op0=ALU.add))
        dv(lambda e: e.scalar_tensor_tensor(out=ur, in0=ur, scalar=2.0, in1=e1, op0=ALU.mult, op1=ALU.add))
        dv(lambda e: e.scalar_tensor_tensor(out=ui, in0=ui, scalar=2.0, in1=e3, op0=ALU.mult, op1=ALU.mult))

    for _ in range(8):
        sq_u()
    dv(lambda e: e.tensor_scalar(out=ar, in0=ur, scalar1=1.0, scalar2=None, op0=ALU.add))
    dv(lambda e: e.tensor_copy(out=ai, in_=ui))
    den, nr, cre, cim, t1, t2 = (t32() for _ in range(6))
    dv(lambda e: e.tensor_tensor(out=den, in0=lr2, in1=lr2, op=ALU.mult))
    dv(lambda e: e.tensor_tensor(out=t1, in0=li2, in1=li2, op=ALU.mult))
    dv(lambda e: e.tensor_tensor(out=den, in0=den, in1=t1, op=ALU.add))
    dv(lambda e: e.reciprocal(out=den, in_=den))
    dv(lambda e: e.tensor_scalar(out=nr, in0=ar, scalar1=-1.0, scalar2=None, op0=ALU.add))
    dv(lambda e: e.tensor_tensor(out=t1, in0=nr, in1=lr2, op=ALU.mult))
    dv(lambda e: e.tensor_tensor(out=t2, in0=ai, in1=li2, op=ALU.mult))
    dv(lambda e: e.tensor_tensor(out=t1, in0=t1, in1=t2, op=ALU.add))
    dv(lambda e: e.tensor_tensor(out=cre, in0=t1, in1=den, op=ALU.mult))
    dv(lambda e: e.tensor_tensor(out=t1, in0=ai, in1=lr2, op=ALU.mult))
    dv(lambda e: e.tensor_tensor(out=t2, in0=nr, in1=li2, op=ALU.mult))
    dv(lambda e: e.tensor_tensor(out=t1, in0=t1, in1=t2, op=ALU.subtract))
    dv(lambda e: e.tensor_tensor(out=cim, in0=t1, in1=den, op=ALU.mult))
    bbr = AL.get([32, 16], F32)
    bbi = AL.get([32, 16], F32)
    tb = AL.get([32, 16], F32)
    X1 = AL.get([32, 16], F32)
    X2 = AL.get([32, 16], F32)
    Z1 = AL.get([32, 16], F32)
    Z2 = AL.get([32, 16], F32)

    def bc16(a):
        return a.unsqueeze(2).to_broadcast([128, 32, 16])

    dv(lambda e: e.tensor_tensor(out=bbr, in0=Bre2, in1=bc16(cre), op=ALU.mult))
    dv(lambda e: e.tensor_tensor(out=tb, in0=Bim2, in1=bc16(cim), op=ALU.mult))
    dv(lambda e: e.tensor_tensor(out=bbr, in0=bbr, in1=tb, op=ALU.subtract))
    dv(lambda e: e.tensor_tensor(out=bbi, in0=Bim2, in1=bc16(cre), op=ALU.mult))
    dv(lambda e: e.tensor_tensor(out=tb, in0=Bre2, in1=bc16(cim), op=ALU.mult))
    dv(lambda e: e.tensor_tensor(out=bbi, in0=bbi, in1=tb, op=ALU.add))
    lo, hi = slice(0, 64), slice(64, 128)
    dv(lambda e: e.tensor_copy(out=X1[lo], in_=bbr[lo]))
    dv(lambda e: e.tensor_copy(out=X1[hi], in_=bbi[hi]))
    dv(lambda e: e.tensor_scalar(out=X2[lo], in0=bbi[lo], scalar1=-1.0, scalar2=None, op0=ALU.mult))
    dv(lambda e: e.tensor_copy(out=X2[hi], in_=bbr[hi]))
    dv(lambda e: e.tensor_copy(out=Z1[lo], in_=Cre2[lo]))
    dv(lambda e: e.tensor_scalar(out=Z1[hi], in0=Cim2[hi], scalar1=-1.0, scalar2=None, op0=ALU.mult))
    dv(lambda e: e.tensor_scalar(out=Z2[lo], in0=Cim2[lo], scalar1=-1.0, scalar2=None, op0=ALU.mult))
    dv(lambda e: e.tensor_scalar(out=Z2[hi], in0=Cre2[hi], scalar1=-1.0, scalar2=None, op0=ALU.mult))
    Pr = AL.get([16, 32], F32)
    Pi_ = AL.get([16, 32], F32)
    inr, ini, m2 = t32(), t32(), t32()
    dv(lambda e: e.tensor_tensor(out=m2, in0=ar, in1=ar, op=ALU.mult))
    dv(lambda e: e.tensor_tensor(out=t1, in0=ai, in1=ai, op=ALU.mult))
    dv(lambda e: e.tensor_tensor(out=m2, in0=m2, in1=t1, op=ALU.add))
    dv(lambda e: e.reciprocal(out=m2, in_=m2))
    dv(lambda e: e.tensor_tensor(out=inr, in0=ar, in1=m2, op=ALU.mult))
    dv(lambda e: e.tensor_tensor(out=ini, in0=ai, in1=m2, op=ALU.mult))
    dv(lambda e: e.tensor_scalar(out=ini, in0=ini, scalar1=-1.0, scalar2=None, op0=ALU.mult))
    dv(lambda e: e.memset(Pr[:, 7, :], 1.0))
    dv(lambda e: e.memset(Pi_[:, 7, :], 0.0))

    def cmul(orr, oii, xr, xi, yr, yi):
        dv(lambda e: e.tensor_tensor(out=t1, in0=xr, in1=yr, op=ALU.mult))
        dv(lambda e: e.tensor_tensor(out=t2, in0=xi, in1=yi, op=ALU.mult))
        dv(lambda e: e.tensor_tensor(out=orr, in0=t1, in1=t2, op=ALU.subtract))
        dv(lambda e: e.tensor_tensor(out=t1, in0=xr, in1=yi, op=ALU.mult))
        dv(lambda e: e.tensor_tensor(out=t2, in0=xi, in1=yr, op=ALU.mult))
        dv(lambda e: e.tensor_tensor(out=oii, in0=t1, in1=t2, op=ALU.add))

    for k in range(1, 9):
        cmul(Pr[:, 7 + k, :], Pi_[:, 7 + k, :], Pr[:, 6 + k, :], Pi_[:, 6 + k, :], ar, ai)
    for k in range(1, 8):
        cmul(Pr[:, 7 - k, :], Pi_[:, 7 - k, :], Pr[:, 8 - k, :], Pi_[:, 8 - k, :], inr, ini)

    MR = AL.get([32, 128], F32)
    MRm = AL.get([32, 128], F32)
    Om = AL.get([32, 128], F32)
    O_all = outs["O_all"]
    tm = AL.get([32, 16], F32)
    tmT = AL.get([128], F32)

    def wz(dst, kidx, A1, A2):
        dv(lambda e: e.tensor_tensor(out=tm, in0=A1, in1=bc16(Pr[:, kidx, :]), op=ALU.mult))
        dv(lambda e: e.tensor_tensor(out=dst, in0=A2, in1=bc16(Pi_[:, kidx, :]), op=ALU.mult))
        dv(lambda e: e.tensor_tensor(out=dst, in0=dst, in1=tm, op=ALU.add))

    for s in range(8):
        wz(MR[:, :, 16 * s:16 * s + 16], 7 + (7 - s), X1, X2)
        wz(MRm[:, :, 16 * s:16 * s + 16], 7 - s, X1, X2)
        wz(Om[:, :, 16 * s:16 * s + 16], 7 + s, Z1, Z2)
        wz(O_all[:, :, 16 * s:16 * s + 16], 7 + s + 1, Z1, Z2)
    R_all, T_all = outs["R_all"], outs["T_all"]
    for g in range(32):
        pb = g % 2
        run(PE, lambda e, g=g, pb=pb: e.transpose(out=psf[:, pb, 0:128], in_=MR[:, g, :], identity=ident_f),
            reads=[G, B_const], writes=[PSF[pb]])
        run(ACT, lambda e, g=g, pb=pb: e.activation(out=R_all[:, g, :], in_=psf[:, pb, 0:128], func=AF.Copy),
            reads=[PSF[pb]], writes=[G], acc=True)
        pt = 2 + g % 2
        run(PE, lambda e, g=g, pt=pt: e.matmul(psf[:, pt, 0:128], MRm[:, g, :], Om[:, g, :], start=True, stop=True),
            reads=[G], writes=[PSF[pt]])
        run(DVE, lambda e, g=g, pt=pt: e.tensor_tensor(out=tmT, in0=psf[:, pt, 0:128], in1=maskT, op=ALU.mult),
            reads=[PSF[pt], G], writes=[GT_])
        run(DVE, lambda e, g=g: e.scalar_tensor_tensor(out=T_all[:, g, :], in0=ident_f, scalar=dcol[:, g:g + 1], in1=tmT,
                                                        op0=ALU.mult, op1=ALU.add),
            reads=[GT_, G, B_const], writes=[G], acc=True)
    D_all, S_f = outs["D_all"], outs["S_f"]
    p16r, p16i, p24r, p24i, c2 = (t32() for _ in range(5))
    cmul(p16r, p16i, Pr[:, 15, :], Pi_[:, 15, :], Pr[:, 15, :], Pi_[:, 15, :])
    cmul(p24r, p24i, p16r, p16i, Pr[:, 15, :], Pi_[:, 15, :])
    tmD = AL.get([128], F32)
    for k, (br_, bi_) in enumerate(((Pr[:, 15, :], Pi_[:, 15, :]), (p16r, p16i), (p24r, p24i))):
        dv(lambda e, bi_=bi_: e.tensor_copy(out=c2[lo], in_=bi_[lo]))
        dv(lambda e, bi_=bi_: e.tensor_scalar(out=c2[hi], in0=bi_[hi], scalar1=-1.0, scalar2=None, op0=ALU.mult))
        for g in range(32):
            dv(lambda e, g=g: e.tensor_scalar(out=tmD, in0=S_f, scalar1=c2[:, g:g + 1], scalar2=None, op0=ALU.mult))
            dv(lambda e, g=g, k=k, br_=br_: e.scalar_tensor_tensor(out=D_all[:, k, g, :], in0=ident_f, scalar=br_[:, g:g + 1],
                                                                    in1=tmD, op0=ALU.mult, op1=ALU.add))
    AA, BB = outs["AA"], outs["BB"]
    for _ in range(5):
        sq_u()
    A8r, A8i = t32(), ui
    dv(lambda e: e.tensor_scalar(out=A8r, in0=ur, scalar1=1.0, scalar2=None, op0=ALU.add))
    dv(lambda e: e.tensor_copy(out=AA[:, 0:32], in_=A8r))
    dv(lambda e: e.tensor_copy(out=AA[:, 32:64], in_=A8r))
    dv(lambda e: e.tensor_scalar(out=BB[lo, 0:32], in0=A8i[lo], scalar1=-1.0, scalar2=None, op0=ALU.mult))
    dv(lambda e: e.tensor_copy(out=BB[hi, 0:32], in_=A8i[hi]))
    dv(lambda e: e.tensor_copy(out=BB[lo, 32:64], in_=A8i[lo]))
    dv(lambda e: e.tensor_scalar(out=BB[hi, 32:64], in0=A8i[hi], scalar1=-1.0, scalar2=None, op0=ALU.mult))
    return G


_CACHE = {}
LAST_RES = None


def _consts():
    ident = np.eye(128, dtype=np.float32)
    r = np.arange(128)
    mask_toep = (r[None, :] // 16 >= r[:, None] // 16).astype(np.float32)
    mask_ws = (r[:, None] // 64 <= r[None, :] // 64).astype(np.float32)
    return {
        "c_ident_bf": ident.astype(ml_dtypes.bfloat16),
        "c_ident_f": ident,
        "c_mask_toep": mask_toep,
        "c_mask_ws": mask_ws,
        "c_swap_bf": np.roll(ident, 64, axis=1).astype(ml_dtypes.bfloat16),
        "c_swap_f": np.roll(ident, 64, axis=1),
    }


def kernel(**inputs):
    B_out.w, B_out.r = [], []
    nc = build_program()
    x = np.ascontiguousarray(np.asarray(inputs["x"], dtype=np.float32)).reshape(8, NTOK, D)
    shared = {}
    for k, v in inputs.items():
        if k == "x":
            continue
        a = np.asarray(v, dtype=np.float32)
        if k != "norm_f_g":
            a = a[0]
        shared[k] = np.ascontiguousarray(a)
    shared.update(_consts())
    zeros = np.zeros((NTOK, D), np.float32)
    in_maps = []
    for c in range(8):
        m = dict(shared)
        m["x_own"] = x[c]
        m["x_prev"] = x[c - 1] if (c % 2 == 1) else zeros
        in_maps.append(m)
    res = run_bass_kernel_spmd(nc, in_maps, core_ids=list(range(8)))
    global LAST_RES
    LAST_RES = res.results
    outs = [np.asarray(r["out"], dtype=np.float32) for r in res.results]
    return np.stack(outs, 0).reshape(4, 8192, D)
```

```python
import os
import numpy as np
import ml_dtypes
import concourse.bass as bass
import concourse.mybir as mybir
from concourse.bass_utils import run_bass_kernel_spmd

F32 = mybir.dt.float32
BF16 = mybir.dt.bfloat16
AF = mybir.ActivationFunctionType
ALU = mybir.AluOpType

D = 1024
NTOK = 4096
FF = 2816
NF = 22
EPS = 1e-6
S5_ON = os.environ.get("K_S5", "1") == "1"
NTILES = int(os.environ.get("K_NTILES", "8"))
TWO_PI = 2.0 * np.pi


class Tok:
    __slots__ = ("sem", "v")

    def __init__(self, sem, v):
        self.sem, self.v = sem, v


class Buf:
    __slots__ = ("w", "r")

    def __init__(self):
        self.w = []
        self.r = []


class Eng:
    def __init__(self, name, nsem, inc, ops=None, seen=None):
        self.name = name
        self.ops = [] if ops is None else ops
        self.seen = {} if seen is None else seen
        self.inc = inc
        self.sems = [f"{name}{i}" for i in range(nsem)]
        self.vals = [0] * nsem
        self.last = [None] * nsem
        self.slot = 0

    def wait(self, tok):
        if tok is None or self.seen.get(tok.sem, 0) >= tok.v:
            return
        self.seen[tok.sem] = tok.v
        self.ops.append(("w", tok.sem, tok.v))

    def emit(self, fn):
        i = self.slot
        self.slot = (self.slot + 1) % len(self.sems)
        if len(self.sems) > 1 and self.last[i] is not None:
            self.wait(self.last[i])
        self.vals[i] += self.inc
        tok = Tok(self.sems[i], self.vals[i])
        self.ops.append(("o", fn, self.sems[i], self.inc))
        self.last[i] = tok
        return tok

    def emit_nosig(self, fn):
        self.ops.append(("o", fn, None, 0))


def run(E, fns, reads=(), writes=(), acc=False, nowait=False):
    if not isinstance(fns, (list, tuple)):
        fns = [fns]
    for b in reads:
        for t in b.w:
            E.wait(t)
    for b in writes:
        if nowait:
            continue
        for t in b.w:
            E.wait(t)
        for t in b.r:
            E.wait(t)
    for f in fns[:-1]:
        E.emit_nosig(f)
    tok = E.emit(fns[-1])
    for b in reads:
        b.r.append(tok)
        if len(b.r) > 64:
            b.r = b.r[-64:]
    for b in writes:
        if acc:
            b.w.append(tok)
        else:
            b.w = [tok]
            b.r = []
    return tok


def build_program():
    nc = bass.Bass("TRN2", target_bir_lowering=False)

    def din(name, shape, dt=F32):
        return nc.dram_tensor(name, list(shape), dt, kind="ExternalInput").ap()

    x_own = din("x_own", [NTOK, D])
    x_prev = din("x_prev", [NTOK, D])
    norm1_g = din("norm1_g", [D])
    w_in = din("w_in", [D, 4608])
    gmlp_ln_g = din("gmlp_ln_g", [D])
    gmlp_ln_b = din("gmlp_ln_b", [D])
    gmlp_ws = din("gmlp_ws", [8, 128, 128])
    gmlp_bs = din("gmlp_bs", [8, 128])
    lam_re = din("s5_lambda_re", [32, 64])
    lam_im = din("s5_lambda_im", [32, 64])
    log_dt = din("s5_log_dt", [32])
    b_re = din("s5_b_re", [32, 64, 16])
    b_im = din("s5_b_im", [32, 64, 16])
    c_re = din("s5_c_re", [32, 16, 64])
    c_im = din("s5_c_im", [32, 16, 64])
    s5_d = din("s5_d", [32, 16])
    w_glu = din("s5_w_glu", [512, 512])
    b_glu = din("s5_b_glu", [512])
    w_a = din("w_branch_a", [D, D])
    w_b = din("w_branch_b", [512, D])
    w_out = din("w_out", [D, D])
    norm2_g = din("norm2_g", [D])
    w_fg = din("w_ffn_gate", [D, FF])
    w_fu = din("w_ffn_up", [D, FF])
    w_fd = din("w_ffn_down", [FF, D])
    norm_f_g = din("norm_f_g", [D])
    c_ident_bf = din("c_ident_bf", [128, 128], BF16)
    c_ident_f = din("c_ident_f", [128, 128])
    c_mask_toep = din("c_mask_toep", [128, 128])
    c_mask_ws = din("c_mask_ws", [128, 128])
    c_swap_bf = din("c_swap_bf", [128, 128], BF16)
    c_swap_f = din("c_swap_f", [128, 128])
    out = nc.dram_tensor("out", [NTOK, D], F32, kind="ExternalOutput").ap()
    DBG = os.environ.get("K_DEBUG", "0") == "1"
    if DBG:
        dbg_yb = nc.dram_tensor("dbg_yb", [128, 16384], BF16, kind="ExternalOutput").ap()
        dbg_h1 = nc.dram_tensor("dbg_h1", [128, 16384], BF16, kind="ExternalOutput").ap()
        dbg_u = nc.dram_tensor("dbg_u", [128, 4096], BF16, kind="ExternalOutput").ap()
        dbg_ys = nc.dram_tensor("dbg_ys", [128, 4096], BF16, kind="ExternalOutput").ap()
        dbg_ycm = nc.dram_tensor("dbg_ycm", [128, 4096], BF16, kind="ExternalOutput").ap()
        dbg_yg = nc.dram_tensor("dbg_yg", [128, 4096], BF16, kind="ExternalOutput").ap()

    win_d = nc.dram_tensor("win_d", [128, 8, 4608], BF16).ap()
    wa_d = nc.dram_tensor("wa_d", [128, 8, D], BF16).ap()
    wb_d = nc.dram_tensor("wb_d", [128, 4, D], BF16).ap()
    wo_d = nc.dram_tensor("wo_d", [128, 8, D], BF16).ap()
    wglu_d = nc.dram_tensor("wglu_d", [128, 4, 512], BF16).ap()
    wfg_d = nc.dram_tensor("wfg_d", [128, 8, FF], BF16).ap()
    wfu_d = nc.dram_tensor("wfu_d", [128, 8, FF], BF16).ap()
    wfd_d = nc.dram_tensor("wfd_d", [128, 8, NF, 128], BF16).ap()

    PE = Eng("pe", 1, 1)
    ACT = Eng("act", 1, 1)
    DVE = Eng("dve", 1, 1)
    POOL = Eng("pool", 1, 1)
    PQ = Eng("pq", 12, 16, ops=POOL.ops, seen=POOL.seen)
    SQ = Eng("sq", 24, 16)
    all_sem_names = PE.sems + ACT.sems + DVE.sems + POOL.sems + PQ.sems + SQ.sems

    ARENA_BYTES = 212800
    arena_cm = nc.sbuf_tensor("arena", [128, ARENA_BYTES // 2], BF16)
    arena = arena_cm.__enter__()
    psf_cm = nc.psum_tensor("psf", [128, 6, 512], F32)
    psf = psf_cm.__enter__()
    psb_cm = nc.psum_tensor("psb", [128, 2, 1024], BF16)
    psb = psb_cm.__enter__()

    class Alloc:
        def __init__(self, base=0):
            self.off = base

        def get(self, shape, dt):
            n = int(np.prod(shape))
            nb = n * (4 if dt == F32 else 2)
            nb = (nb + 63) // 64 * 64
            assert self.off + nb <= ARENA_BYTES, ("arena overflow", self.off + nb)
            a = arena[:, self.off // 2:(self.off + nb) // 2]
            self.off += nb
            if dt == F32:
                a = a.bitcast(F32)
            a = a[:, 0:n]
            if len(shape) == 2:
                return a.rearrange("p (a b) -> p a b", b=shape[1])
            if len(shape) == 3:
                return a.rearrange("p (a b c) -> p a b c", b=shape[1], c=shape[2])
            return a

    PSF = [Buf() for _ in range(6)]
    PSB = [Buf() for _ in range(2)]

    AL = Alloc(0)
    yb = AL.get([4, 4, 1024], BF16)
    YB = [Buf() for _ in range(4)]
    ident_bf = AL.get([128], BF16)
    ident_f = AL.get([128], F32)
    B_const = Buf()
    persist_end = AL.off

    run(PQ, lambda e: e.dma_start(out=ident_bf, in_=c_ident_bf[:, :]), writes=[B_const], acc=True)
    run(PQ, lambda e: e.dma_start(out=ident_f, in_=c_ident_f[:, :]), writes=[B_const], acc=True)

    CONV = {}

    def conv(key, dst, src, ktiles):
        CONV[key] = Buf()
        for k in range(ktiles):
            run(PQ, lambda e, k=k: e.dma_start(out=dst[:, k, :], in_=src[k * 128:(k + 1) * 128, :]),
                writes=[CONV[key]], acc=True, nowait=True)

    CONV["winB"] = Buf()
    for k in range(8):
        run(PQ, lambda e, k=k: e.dma_start(out=win_d[:, k, 2048:2560], in_=w_in[k * 128:(k + 1) * 128, 2048:2560]),
            writes=[CONV["winB"]], acc=True, nowait=True)

    conv_parts = []
    def conv_win():
        CONV["win"] = Buf()
        for k in range(8):
            for c0, c1 in ((0, 2048), (2560, 4608)):
                run(PQ, lambda e, k=k, c0=c0, c1=c1: e.dma_start(out=win_d[:, k, c0:c1],
                                                                 in_=w_in[k * 128:(k + 1) * 128, c0:c1]),
                    writes=[CONV["win"]], acc=True, nowait=True)
    conv_parts.append(conv_win)
    if S5_ON:
        conv_parts.append(lambda: conv("wglu", wglu_d, w_glu, 4))
    conv_parts.append(lambda: conv("wa", wa_d, w_a, 8))
    conv_parts.append(lambda: (conv("wb", wb_d, w_b, 4), conv("wo", wo_d, w_out, 8)))
    conv_parts.append(lambda: conv("wfg", wfg_d, w_fg, 8))
    conv_parts.append(lambda: conv("wfu", wfu_d, w_fu, 8))

    def conv_wfd():
        CONV["wfd"] = Buf()
        for f in range(NF):
            run(PQ, lambda e, f=f: e.dma_start(
                out=wfd_d[:, :, f, :], in_=w_fd[f * 128:(f + 1) * 128, :].rearrange("p (m c) -> p m c", c=128)),
                writes=[CONV["wfd"]], acc=True, nowait=True)
    conv_parts.append(conv_wfd)

    def bulk_conv(n=None):
        k = len(conv_parts) if n is None else n
        for _ in range(k):
            if conv_parts:
                conv_parts.pop(0)()

    H1B = Buf()

    AL = Alloc(persist_end)
    gcols = AL.get([40], F32)
    cst2 = AL.get([4], F32)
    epsc = cst2[:, 0:1]
    B_setup = Buf()
    g1T, g2T, lngT, lnbT, bgluT = gcols[:, 0:8], gcols[:, 8:16], gcols[:, 16:24], gcols[:, 24:32], gcols[:, 32:36]
    gstage = Alloc(200 * 1024).get([128], F32)
    for r0, nk, src in ((0, 8, norm1_g), (8, 8, norm2_g), (16, 8, gmlp_ln_g), (24, 8, gmlp_ln_b), (32, 4, b_glu)):
        run(SQ, lambda e, r0=r0, nk=nk, src=src: e.dma_start(out=gstage[r0:r0 + nk, :],
                                                             in_=src.rearrange("(k p) -> k p", p=128)),
            writes=[B_setup], acc=True)
    run(PE, lambda e: e.transpose(out=psf[:, 0, 0:36], in_=gstage[0:36, :], identity=ident_f[0:36, 0:36]),
        reads=[B_setup, B_const], writes=[PSF[0]])
    run(DVE, lambda e: e.tensor_copy(out=gcols[:, 0:36], in_=psf[:, 0, 0:36]), reads=[PSF[0]], writes=[B_setup], acc=True)
    run(POOL, lambda e: e.memset(epsc, EPS), writes=[B_setup], acc=True)
    persist_end = AL.off

    def barrier(skip_pq=False):
        engs = [PE, ACT, DVE, POOL, SQ]
        toks = []
        for E_ in ((PE, ACT, DVE, POOL, SQ) if skip_pq else (PE, ACT, DVE, POOL, PQ, SQ)):
            toks += [t for t in E_.last if t is not None]
        for E_ in engs:
            for t in toks:
                E_.wait(t)

    if S5_ON:
        AL = Alloc(persist_end)
        R_all = AL.get([32, 128], BF16)
        T_all = AL.get([32, 128], BF16)
        O_all = AL.get([32, 128], BF16)
        D_all = AL.get([3, 32, 128], BF16)
        S_bf = AL.get([128], BF16)
        AA = AL.get([64], F32)
        BB = AL.get([64], F32)
        gen_base = AL.off
        H1 = AL.get([32, 512], BF16)
        H2 = AL.get([32, 128], BF16)
        GG = AL.get([2, 128, 32], BF16)
        G2c = AL.get([2, 32, 64], BF16)
        XS = AL.get([2, 64], F32)
        T1 = AL.get([64], F32)
        T2 = AL.get([64], F32)
        sstat = AL.get([2, 8], F32)
        scratch_base = AL.off
        U_own = yb.rearrange("p s j t -> p s (j t)").rearrange("p s (g c) -> p s g c", c=128)
        dram = dict(lam_re=lam_re, lam_im=lam_im, log_dt=log_dt, b_re=b_re, b_im=b_im, c_re=c_re, c_im=c_im,
                    s5_d=s5_d, c_mask_toep=c_mask_toep)
        S_f = Alloc(ARENA_BYTES - 512).get([128], F32)
        run(SQ, lambda e: e.dma_start(out=S_f, in_=c_swap_f[:, :]), writes=[B_const], acc=True)
        run(SQ, lambda e: e.dma_start(out=S_bf, in_=c_swap_bf[:, :]), writes=[B_const], acc=True)
        G_gen = emit_s5_gen(run, Buf, dict(PE=PE, ACT=ACT, DVE=DVE, POOL=POOL, SQ=SQ), Alloc(gen_base), dram,
                            psf, PSF, ident_f, B_const,
                            dict(R_all=R_all, T_all=T_all, O_all=O_all, AA=AA, BB=BB, D_all=D_all, S_f=S_f))
        barrier(skip_pq=True)
        AL = Alloc(scratch_base)
        xcm = AL.get([1, 8, 1024], BF16)
        XCM = [Buf(), Buf()]
        hTs = AL.get([8, 8, 128], BF16)
        HTS = [Buf() for _ in range(8)]
        wBg = AL.get([8, 512], BF16)
        WBG = Buf()
        XBcm = AL.get([32, 8, 16], BF16)
        XBC = [Buf() for _ in range(8)]
        U_tmp = AL.get([32, 128], BF16)
        UT = Buf()
        junk = AL.get([1024], BF16)
        JK = Buf()
        GGB = [Buf(), Buf()]
        G2B = [Buf(), Buf()]
        H2B = [Buf() for _ in range(4)]
        SSB = [Buf(), Buf()]
        SC = Buf()
        run(SQ, lambda e: e.dma_start(out=wBg, in_=win_d[:, :, 2048:2560]), reads=[CONV["winB"]], writes=[WBG])
        for k in range(8):
            run(ACT, lambda e, k=k: e.activation(out=wBg[:, k, :], in_=wBg[:, k, :], func=AF.Copy, scale=g1T[:, k:k + 1]),
                reads=[B_setup], writes=[WBG])
        run(DVE, lambda e: e.memset(XS[:, 0, :], 0.0), writes=[SC])
        cur = 0
        pending_down = []
        for st8 in range(8):
            own = st8 >= 4
            st = st8 % 4
            xsrc = x_own if own else x_prev
            xb_ = 0
            run(PQ, lambda e, xb_=xb_, xsrc=xsrc, st=st: e.dma_start(
                out=xcm[:, xb_], in_=xsrc[st * 1024:(st + 1) * 1024, :].rearrange("(c s) d -> c s d", s=8)),
                writes=[XCM[xb_]])
            ss = sstat[:, xb_, :]
            bulk_conv(1)
            run(ACT, lambda e, ss=ss: e.memzero(ss), writes=[SSB[xb_]])
            for s_ in range(8):
                run(ACT, lambda e, s_=s_, xb_=xb_, ss=ss: e.activation(out=junk, in_=xcm[:, xb_, s_, :], func=AF.Square,
                                                                        accum_out=ss[:, s_:s_ + 1]),
                    reads=[XCM[xb_]], writes=[JK, SSB[xb_]])
            run(ACT, lambda e, ss=ss: e.activation(out=ss, in_=ss, func=AF.Ln, bias=epsc, scale=1.0 / D),
                reads=[B_setup], writes=[SSB[xb_]])
            run(ACT, lambda e, ss=ss: e.activation(out=ss, in_=ss, func=AF.Exp, scale=-0.5), writes=[SSB[xb_]])
            for s_ in range(8):
                pb = s_ % 2
                run(PE, [lambda e, k=k, s_=s_, pb=pb, xb_=xb_: e.transpose(out=psb[:, pb, k * 128:(k + 1) * 128],
                                                                            in_=xcm[:, xb_, s_, k * 128:(k + 1) * 128],
                                                                            identity=ident_bf) for k in range(8)],
                    reads=[XCM[xb_], B_const], writes=[PSB[pb]])
                run(ACT, lambda e, s_=s_, pb=pb: e.activation(out=hTs[:, :, s_, :],
                                                              in_=psb[:, pb, :].rearrange("p (k c) -> p k c", c=128),
                                                              func=AF.Copy),
                    reads=[PSB[pb]], writes=[HTS[s_]])
            for s_ in range(8):
                pb = s_ % 4
                run(PE, [lambda e, k=k, s_=s_, pb=pb: e.matmul(psf[:, pb, :], hTs[:, k, s_, :], wBg[:, k, :],
                                                                start=(k == 0), stop=(k == 7)) for k in range(8)],
                    reads=[HTS[s_], WBG], writes=[PSF[pb]])
                run(ACT, lambda e, s_=s_, pb=pb, ss=ss: e.activation(
                    out=XBcm[:, :, s_, :], in_=psf[:, pb, :].rearrange("p (g h) -> p g h", h=16), func=AF.Copy,
                    scale=ss[:, s_:s_ + 1]),
                    reads=[PSF[pb], SSB[xb_]], writes=[XBC[s_]])
            Ud = U_own[:, st] if own else U_tmp
            UB = YB[st] if own else UT
            for g4 in range(8):
                pb = g4 % 2
                run(PE, [lambda e, gi=gi, g4=g4, pb=pb: e.transpose(
                    out=psb[:, pb, gi * 128:(gi + 1) * 128],
                    in_=XBcm[:, 4 * g4 + gi, :, :].rearrange("p s h -> p (s h)"), identity=ident_bf) for gi in range(4)],
                    reads=XBC + [B_const], writes=[PSB[pb]])
                run(ACT, lambda e, g4=g4, pb=pb, Ud=Ud: e.activation(
                    out=Ud[:, 4 * g4:4 * g4 + 4, :], in_=psb[:, pb, 0:512].rearrange("p (g c) -> p g c", c=128), func=AF.Copy),
                    reads=[PSB[pb]], writes=[UB], acc=(g4 > 0))
            gb_ = st8 % 2
            GGv = GG[:, gb_]
            G2v = G2c[:, gb_]
            for g4 in range(8):
                pb = 4 + g4 % 2
                run(PE, [lambda e, gi=gi, g4=g4, pb=pb, Ud=Ud: e.matmul(
                    psf[:, pb, gi * 128:(gi + 1) * 128], R_all[:, 4 * g4 + gi, :], Ud[:, 4 * g4 + gi, :],
                    start=True, stop=True) for gi in range(4)],
                    reads=[UB, G_gen], writes=[PSF[pb]])
                run(ACT, lambda e, g4=g4, pb=pb, GGv=GGv: e.activation(
                    out=GGv[:, :, 4 * g4:4 * g4 + 4],
                    in_=psf[:, pb, :].rearrange("p (g c) -> p c g", c=128), func=AF.Copy),
                    reads=[PSF[pb]], writes=[GGB[gb_]], acc=(g4 > 0))
            for gh in range(2):
                pb = 4 + gh
                fns = []
                for gl in range(16):
                    g = gh * 16 + gl
                    for i in range(4):
                        lw = ident_bf if i == 3 else D_all[:, 2 - i, g, :]
                        fns.append(lambda e, g=g, gl=gl, i=i, pb=pb, lw=lw, GGv=GGv: e.matmul(
                            psf[:, pb, gl * 32:(gl + 1) * 32], lw,
                            GGv.rearrange("p (C i) g -> p C i g", i=4)[:, :, i, g], start=(i == 0), stop=(i == 3)))
                run(PE, fns, reads=[GGB[gb_], G_gen, B_const], writes=[PSF[pb]])
                run(ACT, lambda e, gh=gh, pb=pb, G2v=G2v: e.activation(
                    out=G2v[:, :, 16 * gh:16 * gh + 16], in_=psf[:, pb, :].rearrange("p (g c) -> p c g", c=32),
                    func=AF.Copy), reads=[PSF[pb]], writes=[G2B[gb_]], acc=(gh > 0))
            for hh in range(2):
                pb = 4 + hh
                run(PE, [lambda e, C=C, hh=hh, pb=pb, G2v=G2v: e.matmul(
                    psf[:, pb, (C % 16) * 32:(C % 16 + 1) * 32], S_bf, G2v[:, C, 0:32], start=True, stop=True)
                    for C in range(16 * hh, 16 * hh + 16)],
                    reads=[G2B[gb_], B_const], writes=[PSF[pb]])
                run(ACT, lambda e, hh=hh, pb=pb, G2v=G2v: e.activation(
                    out=G2v[:, 16 * hh:16 * hh + 16, 32:64], in_=psf[:, pb, :].rearrange("p (c g) -> p c g", g=32),
                    func=AF.Copy), reads=[PSF[pb]], writes=[G2B[gb_]], acc=True)
            if DBG and st8 == 4:
                run(SQ, lambda e, Ud=Ud: e.dma_start(out=dbg_u[:, :], in_=Ud.rearrange("p g c -> p (g c)")), reads=[UB])
            while pending_down:
                pending_down.pop(0)()
            for C in range(32):
                Xc = XS[:, cur, :]
                Xn = XS[:, 1 - cur, :]
                if own:
                    run(DVE, lambda e, Xc=Xc, C=C, st=st: e.tensor_copy(out=H2[:, :, st * 32 + C], in_=Xc[:, 0:32]),
                        reads=[SC], writes=[H2B[st]])
                run(DVE, lambda e, Xc=Xc: e.tensor_tensor(out=T1, in0=AA, in1=Xc, op=ALU.mult), reads=[G_gen], writes=[SC])
                run(DVE, lambda e, Xc=Xc: e.tensor_tensor(out=T2[:, 0:32], in0=BB[:, 0:32], in1=Xc[:, 32:64], op=ALU.mult),
                    writes=[SC])
                run(DVE, lambda e, Xc=Xc: e.tensor_tensor(out=T2[:, 32:64], in0=BB[:, 32:64], in1=Xc[:, 0:32], op=ALU.mult),
                    writes=[SC])
                run(DVE, lambda e: e.tensor_tensor(out=T1, in0=T1, in1=T2, op=ALU.add), writes=[SC])
                run(DVE, lambda e, Xn=Xn, C=C, G2v=G2v: e.tensor_tensor(out=Xn, in0=T1, in1=G2v[:, C, :], op=ALU.add),
                    reads=[G2B[gb_]], writes=[SC])
                cur = 1 - cur
            def emit_down(st=st, GGv=GGv, gb_=gb_):
                for g4 in range(8):
                    pb = g4 % 4
                    fns = []
                    for gi in range(4):
                        g = 4 * g4 + gi
                        hsrc = H2[:, g, st * 32:(st + 1) * 32]
                        gsrc = GGv.rearrange("p (C i) g -> p C i g", i=4)
                        for j in range(4):
                            terms = [(ident_bf if j == 0 else D_all[:, j - 1, g, :], hsrc)]
                            for i in range(j):
                                kpow = j - 1 - i
                                terms.append((ident_bf if kpow == 0 else D_all[:, kpow - 1, g, :], gsrc[:, :, i, g]))
                            for ti, (lw, rh) in enumerate(terms):
                                fns.append(lambda e, gi=gi, j=j, pb=pb, lw=lw, rh=rh, ti=ti, nt=len(terms): e.matmul(
                                    psf[:, pb, gi * 128 + j * 32:gi * 128 + (j + 1) * 32], lw, rh,
                                    start=(ti == 0), stop=(ti == nt - 1)))
                    run(PE, fns, reads=[H2B[st], GGB[gb_], G_gen, B_const], writes=[PSF[pb]])
                    run(ACT, lambda e, g4=g4, pb=pb, st=st: e.activation(
                        out=H1[:, 4 * g4:4 * g4 + 4, st * 128:(st + 1) * 128].rearrange("p g (c j) -> p g j c", j=4),
                        in_=psf[:, pb, :].rearrange("p (g j c) -> p g j c", j=4, c=32), func=AF.Copy),
                        reads=[PSF[pb]], writes=[H1B], acc=True)
            if own:
                pending_down.append(emit_down)
        while pending_down:
            pending_down.pop(0)()
        barrier()
        AL = Alloc(scratch_base)
        Ys2d_ = AL.get([2, 32, 128], BF16)
        YS_ = [Buf(), Buf()]
        Ycm_ = AL.get([2, 8, 512], BF16)
        YC_ = [Buf(), Buf()]
        yg_ = AL.get([2, 4, 1024], BF16)
        YG_ = [Buf(), Buf()]
        wglu = AL.get([4, 512], BF16)
        WGL = Buf()
        sgl = AL.get([2, 512], F32)
        SGL = [Buf(), Buf()]
        run(SQ, lambda e: e.dma_start(out=wglu, in_=wglu_d[:, :, :]), reads=[CONV["wglu"]], writes=[WGL])
        for st in range(4):
            Ys2d, YS, Ycm, YC, yg, YG = Ys2d_[:, st % 2], YS_[st % 2], Ycm_[:, st % 2], YC_[st % 2], yg_[:, st % 2], YG_[st % 2]
            for g4 in range(8):
                pb = g4 % 4
                fns = []
                for gi in range(4):
                    g = 4 * g4 + gi
                    fns.append(lambda e, Ys2d=Ys2d, Ycm=Ycm, yg=yg, g=g, gi=gi, pb=pb, st=st: e.matmul(psf[:, pb, gi * 128:(gi + 1) * 128], T_all[:, g, :],
                                                                     U_own[:, st, g, :], start=True, stop=False))
                    fns.append(lambda e, Ys2d=Ys2d, Ycm=Ycm, yg=yg, g=g, gi=gi, pb=pb, st=st: e.matmul(psf[:, pb, gi * 128:(gi + 1) * 128], O_all[:, g, :],
                                                                     H1[:, g, st * 128:(st + 1) * 128], start=False, stop=True))
                run(PE, fns, reads=[YB[st], H1B, G_gen], writes=[PSF[pb]])
                run(DVE, lambda e, Ys2d=Ys2d, Ycm=Ycm, yg=yg, g4=g4, pb=pb: e.tensor_copy(out=Ys2d[:, 4 * g4:4 * g4 + 4, :],
                                                               in_=psf[:, pb, :].rearrange("p (g c) -> p g c", c=128)),
                    reads=[PSF[pb]], writes=[YS], acc=(g4 > 0))
            for g4 in range(8):
                pb = g4 % 2
                run(PE, [lambda e, Ys2d=Ys2d, Ycm=Ycm, yg=yg, gi=gi, g4=g4, pb=pb: e.transpose(out=psb[:, pb, gi * 128:(gi + 1) * 128],
                                                                    in_=Ys2d[:, 4 * g4 + gi, :], identity=ident_bf)
                         for gi in range(4)], reads=[YS, B_const], writes=[PSB[pb]])
                run(DVE, lambda e, Ys2d=Ys2d, Ycm=Ycm, yg=yg, g4=g4, pb=pb: e.tensor_copy(
                    out=Ycm[:, :, 64 * g4:64 * g4 + 64].rearrange("p t (g h) -> p t g h", h=16),
                    in_=psb[:, pb, 0:512].rearrange("p (g t h) -> p t g h", t=8, h=16)),
                    reads=[PSB[pb]], writes=[YC], acc=(g4 > 0))
            for tau in range(8):
                pb = tau % 2
                run(PE, [lambda e, Ys2d=Ys2d, Ycm=Ycm, yg=yg, j=j, tau=tau, pb=pb: e.transpose(out=psb[:, pb, j * 128:(j + 1) * 128],
                                                                    in_=Ycm[:, tau, j * 128:(j + 1) * 128], identity=ident_bf)
                         for j in range(4)], reads=[YC, B_const], writes=[PSB[pb]])
                run(ACT, lambda e, Ys2d=Ys2d, Ycm=Ycm, yg=yg, tau=tau, pb=pb: e.activation(
                    out=yg.rearrange("p j (c s) -> p j c s", s=8)[:, :, :, tau],
                    in_=psb[:, pb, 0:512].rearrange("p (j c) -> p j c", c=128), func=AF.Gelu_apprx_tanh),
                    reads=[PSB[pb]], writes=[YG], acc=(tau > 0))
            for j2 in range(4):
                for hf in range(2):
                    pb = 4 + hf
                    run(PE, [lambda e, Ys2d=Ys2d, Ycm=Ycm, yg=yg, j=j, j2=j2, hf=hf, pb=pb: e.matmul(psf[:, pb, :], wglu[:, j, j2 * 128:(j2 + 1) * 128],
                                                                          yg[:, j, hf * 512:(hf + 1) * 512],
                                                                          start=(j == 0), stop=(j == 3)) for j in range(4)],
                        reads=[YG, WGL], writes=[PSF[pb]])
                    run(ACT, lambda e, Ys2d=Ys2d, Ycm=Ycm, yg=yg, j2=j2, hf=hf, pb=pb: e.activation(out=sgl[:, hf, :], in_=psf[:, pb, :], func=AF.Sigmoid,
                                                                          bias=bgluT[:, j2:j2 + 1], scale=1.0),
                        reads=[PSF[pb], B_setup], writes=[SGL[hf]])
                    run(DVE, lambda e, Ys2d=Ys2d, Ycm=Ycm, yg=yg, j2=j2, hf=hf, st=st: e.tensor_tensor(out=yb[:, st, j2, hf * 512:(hf + 1) * 512],
                                                                            in0=yg[:, j2, hf * 512:(hf + 1) * 512],
                                                                            in1=sgl[:, hf, :], op=ALU.mult),
                        reads=[YG, SGL[hf]], writes=[YB[st]], acc=not (j2 == 0 and hf == 0))
        barrier()
        if DBG:
            run(SQ, lambda e: e.dma_start(out=dbg_yb[:, :], in_=yb.rearrange("p s j t -> p (s j t)")), writes=[Buf()])
            run(SQ, lambda e: e.dma_start(out=dbg_h1[:, :], in_=H1.rearrange("p g c -> p (g c)")), writes=[Buf()])
            run(SQ, lambda e: e.dma_start(out=dbg_ys[:, :], in_=Ys2d.rearrange("p g c -> p (g c)")), writes=[Buf()])
            run(SQ, lambda e: e.dma_start(out=dbg_ycm[:, :], in_=Ycm.rearrange("p t c -> p (t c)")), writes=[Buf()])
            run(SQ, lambda e: e.dma_start(out=dbg_yg[:, :], in_=yg.rearrange("p j t -> p (j t)")), writes=[Buf()])
            barrier()
    else:
        bulk_conv()
        for st in range(4):
            run(POOL, lambda e, st=st: e.memset(yb[:, st], 0.0), writes=[YB[st]])

    bulk_conv()
    AL = Alloc(persist_end)
    xb2 = AL.get([2, 4, D], F32)
    X2 = [[Buf() for _ in range(4)] for _ in range(2)]
    ostage = AL.get([1, D], F32)
    OST = [Buf()]
    hTa = AL.get([8, 512], BF16)
    HTa = [Buf() for _ in range(4)]
    hTb = AL.get([8, 512], BF16)
    HTb = [Buf() for _ in range(4)]
    hb = AL.get([2, D], BF16)
    HB = [Buf(), Buf()]
    vg = AL.get([1, D], F32)
    VG = [Buf()]
    nb_ = AL.get([2, D], BF16)
    NB = [Buf() for _ in range(2)]
    mx = AL.get([8, 512], BF16)
    MX = [Buf() for _ in range(4)]
    scr = AL.get([2, 512], F32)
    SCR = [Buf() for _ in range(2)]
    ya = AL.get([8, 512], BF16)
    YA = [Buf() for _ in range(8)]
    gts = AL.get([2, 512], F32)
    GT = [Buf() for _ in range(2)]
    mg = AL.get([8, 512], BF16)
    MG = [Buf() for _ in range(8)]
    act_off = AL.off
    act = AL.get([NF, 512], BF16)
    AC = [Buf() for _ in range(NF)]
    gfb = AL.get([D], F32)
    biasT = AL.get([8, 128], BF16)
    wsT = AL.get([8, 128], BF16)
    stats = AL.get([64], F32)
    AL2 = Alloc(act_off)
    wstage = AL2.get([8, 128], F32)
    bsb = AL2.get([8, 128], F32)
    maskws = AL2.get([128], F32)
    ones_bf = AL2.get([128], BF16)
    NRING = 6
    ring = AL.get([NRING, 4096], BF16)
    RING = [Buf() for _ in range(NRING)]
    STATB = [Buf() for _ in range(4)]

    run(SQ, lambda e: e.dma_start(out=gfb, in_=norm_f_g.partition_broadcast(128)), writes=[B_setup], acc=True)
    run(SQ, lambda e: e.dma_start(out=bsb.rearrange("p g i -> p (g i)"),
                                  in_=gmlp_bs.rearrange("g i -> (g i)").partition_broadcast(128)),
        writes=[B_setup], acc=True)
    run(SQ, lambda e: e.dma_start(out=wstage, in_=gmlp_ws.rearrange("g i j -> i g j")),
        writes=[B_setup], acc=True)
    run(SQ, lambda e: e.dma_start(out=maskws, in_=c_mask_ws[:, :]), writes=[B_setup], acc=True)
    for g in range(8):
        pb = g % 6
        run(PE, lambda e, g=g, pb=pb: e.transpose(out=psf[:, pb, 0:128], in_=wstage[:, g, :], identity=ident_f),
            reads=[B_setup, B_const], writes=[PSF[pb]])
        run(DVE, lambda e, g=g, pb=pb: e.tensor_tensor(out=wsT[:, g, :], in0=psf[:, pb, 0:128], in1=maskws, op=ALU.mult),
            reads=[PSF[pb], B_setup], writes=[B_setup], acc=True)
    run(POOL, lambda e: e.memset(ones_bf, 1.0), writes=[B_setup], acc=True)
    for h2 in range(2):
        run(PE, lambda e, h2=h2: e.matmul(psf[:, h2, :], ones_bf, wsT[:, 4 * h2:4 * h2 + 4, :].rearrange("p g i -> p (g i)"),
                                          start=True, stop=True),
            reads=[B_setup], writes=[PSF[h2]])
        for gg in range(4):
            g = 4 * h2 + gg
            run(DVE, lambda e, g=g, gg=gg, h2=h2: e.scalar_tensor_tensor(
                out=biasT[:, g, :], in0=psf[:, h2, gg * 128:(gg + 1) * 128], scalar=lnbT[:, g:g + 1],
                in1=bsb[:, g, :], op0=ALU.mult, op1=ALU.add),
                reads=[PSF[h2], B_setup], writes=[B_setup], acc=True)

    for f in range(NF):
        AC[f].r = list(B_setup.w) + list(B_setup.r)

    ring_i = [0]

    def wload(src_ap, shape3, conv_key):
        i = ring_i[0]
        ring_i[0] = (i + 1) % NRING
        a, b = shape3
        dst = ring[:, i, 0:a * b].rearrange("p (a b) -> p a b", b=b)
        run(SQ, lambda e: e.dma_start(out=dst, in_=src_ap), reads=[CONV[conv_key]], writes=[RING[i]])
        return dst, RING[i]

    psf_i = [0]

    def next_psf():
        i = psf_i[0]
        psf_i[0] = (i + 1) % 6
        return i

    scr_i = [0]

    def next_scr():
        i = scr_i[0]
        scr_i[0] = (i + 1) % 2
        return i

    def rms_scale(b, tagbase, xs):
        s = b % 2
        ssq = stats[:, tagbase + b:tagbase + b + 1]
        run(DVE, lambda e, ssq=ssq: e.memset(ssq, 0.0), writes=[STATB[b]])
        run(ACT, lambda e, b=b, s=s, ssq=ssq, xs=xs: e.activation(out=hb[:, s, :], in_=xb2[:, xs, b, :], func=AF.Square,
                                                                   accum_out=ssq),
            reads=[X2[xs][b]], writes=[HB[s], STATB[b]])
        run(ACT, lambda e, ssq=ssq: e.activation(out=ssq, in_=ssq, func=AF.Sqrt, bias=epsc, scale=1.0 / D),
            reads=[B_setup], writes=[STATB[b]])
        run(DVE, lambda e, ssq=ssq: e.reciprocal(out=ssq, in_=ssq), writes=[STATB[b]])
        run(DVE, lambda e, b=b, s=s, ssq=ssq, xs=xs: e.tensor_scalar(out=hb[:, s, :], in0=xb2[:, xs, b, :], scalar1=ssq,
                                                                      scalar2=None, op0=ALU.mult),
            reads=[X2[xs][b], STATB[b]], writes=[HB[s]])

    def rms_transpose(b, gT, hTd, HTd):
        s = b % 2
        pb = b % 2
        run(PE, [lambda e, k=k, s=s, pb=pb: e.transpose(out=psb[:, pb, k * 128:(k + 1) * 128],
                                                        in_=hb[:, s, k * 128:(k + 1) * 128], identity=ident_bf)
                 for k in range(8)],
            reads=[HB[s], B_const], writes=[PSB[pb]])
        run(DVE, lambda e, b=b, pb=pb, hTd=hTd, gT=gT: e.tensor_tensor(
            out=hTd[:, :, b * 128:(b + 1) * 128], in0=psb[:, pb, :].rearrange("p (k t) -> p k t", t=128),
            in1=gT.unsqueeze(2).to_broadcast([128, 8, 128]), op=ALU.mult),
            reads=[PSB[pb], B_setup], writes=[HTd[b]])

    def rms_to_hT(gT, tagbase, xs, hTd, HTd):
        rms_scale(0, tagbase, xs)
        rms_scale(1, tagbase, xs)
        rms_transpose(0, gT, hTd, HTd)
        rms_scale(2, tagbase, xs)
        rms_transpose(1, gT, hTd, HTd)
        rms_scale(3, tagbase, xs)
        rms_transpose(2, gT, hTd, HTd)
        rms_transpose(3, gT, hTd, HTd)

    def fm_to_x(src, SRC, xs, blocks=(0, 1, 2, 3)):
        for b in blocks:
            pb = b % 2
            run(PE, [lambda e, m=m, b=b, pb=pb: e.transpose(out=psb[:, pb, m * 128:(m + 1) * 128],
                                                            in_=src[:, m, b * 128:(b + 1) * 128], identity=ident_bf)
                     for m in range(8)],
                reads=list(SRC) + [B_const], writes=[PSB[pb]])
            run(DVE, lambda e, b=b, pb=pb, xs=xs: e.tensor_tensor(out=xb2[:, xs, b, :], in0=xb2[:, xs, b, :], in1=psb[:, pb, :],
                                                                   op=ALU.add),
                reads=[PSB[pb]], writes=[X2[xs][b]])

    def load_x(tt):
        for b in range(4):
            run(PQ, lambda e, b=b, tt=tt: e.dma_start(out=xb2[:, tt % 2, b, :],
                                                      in_=x_own[tt * 512 + b * 128:tt * 512 + (b + 1) * 128, :]),
                writes=[X2[tt % 2][b]])

    load_x(0)
    rms_to_hT(g1T, 40, 0, hTa, HTa)

    for tt in range(NTILES):
        t0 = tt * 512
        xs = tt % 2
        st_, so_ = tt // 2, (tt % 2) * 512
        if tt + 1 < NTILES:
            load_x(tt + 1)
        wv0, WV0 = wload(win_d[:, :, 1024:1536], (8, 512), "win")
        wv1, WV1 = wload(win_d[:, :, 1536:2048], (8, 512), "win")
        fb = [0]

        def fbank():
            fb[0] ^= 1
            return 4 + fb[0]

        def v_block(b):
            s = 0
            for hf, (wv, WV) in enumerate(((wv0, WV0), (wv1, WV1))):
                run(PE, [lambda e, k=k, b=b, hf=hf, wv=wv: e.matmul(psf[:, hf, :], hTa[:, k, b * 128:(b + 1) * 128],
                                                                     wv[:, k, :], start=(k == 0), stop=(k == 7))
                         for k in range(8)],
                    reads=[HTa[b], WV], writes=[PSF[hf]])
            s1 = stats[:, 8 + 2 * b:9 + 2 * b]
            s2 = stats[:, 9 + 2 * b:10 + 2 * b]
            mean = stats[:, 16 + 2 * b:17 + 2 * b]
            rstd = stats[:, 17 + 2 * b:18 + 2 * b]
            run(DVE, lambda e, b=b: e.memset(stats[:, 8 + 2 * b:10 + 2 * b], 0.0), writes=[STATB[b]])
            run(ACT, lambda e, s=s, s1=s1: e.activation(out=vg[:, s, :], in_=psf[:, 0:2, :].rearrange("p a b -> p (a b)"),
                                                         func=AF.Gelu_apprx_tanh, accum_out=s1),
                reads=[PSF[0], PSF[1]], writes=[VG[s], STATB[b]])
            sn = b % 2
            run(ACT, lambda e, s=s, sn=sn, s2=s2: e.activation(out=nb_[:, sn, :], in_=vg[:, s, :], func=AF.Square, accum_out=s2),
                reads=[VG[s]], writes=[NB[sn], STATB[b]])
            run(DVE, lambda e, mean=mean, s1=s1: e.tensor_scalar(out=mean, in0=s1, scalar1=1.0 / D, scalar2=None, op0=ALU.mult),
                writes=[STATB[b]])
            run(DVE, lambda e, mean=mean, rstd=rstd: e.tensor_tensor(out=rstd, in0=mean, in1=mean, op=ALU.mult),
                writes=[STATB[b]])
            run(DVE, lambda e, s2=s2, rstd=rstd: e.scalar_tensor_tensor(out=rstd, in0=s2, scalar=1.0 / D, in1=rstd,
                                                                         op0=ALU.mult, op1=ALU.subtract),
                writes=[STATB[b]])
            run(ACT, lambda e, rstd=rstd: e.activation(out=rstd, in_=rstd, func=AF.Sqrt, bias=epsc, scale=1.0),
                reads=[B_setup], writes=[STATB[b]])
            run(DVE, lambda e, rstd=rstd: e.reciprocal(out=rstd, in_=rstd), writes=[STATB[b]])
            run(DVE, lambda e, s=s, sn=sn, mean=mean, rstd=rstd: e.tensor_scalar(out=nb_[:, sn, :], in0=vg[:, s, :], scalar1=mean,
                                                                                  scalar2=rstd, op0=ALU.subtract, op1=ALU.mult),
                reads=[VG[s], STATB[b]], writes=[NB[sn]])

        def sp_block(b):
            s = b % 2
            run(PE, [lambda e, g=g, s=s: e.matmul(psf[:, 2 + g // 4, (g % 4) * 128:(g % 4 + 1) * 128],
                                                  nb_[:, s, g * 128:(g + 1) * 128], wsT[:, g, :], start=True, stop=True)
                     for g in range(8)],
                reads=[NB[s], B_setup], writes=[PSF[2], PSF[3]])
            run(ACT, lambda e, b=b: e.activation(out=mx[:, :, b * 128:(b + 1) * 128],
                                                 in_=psf[:, 2:4, :].rearrange("p a (g i) -> p (a g) i", i=128),
                                                 func=AF.Copy),
                reads=[PSF[2], PSF[3]], writes=[MX[b]])
            scv = scr.rearrange("p a (b i) -> p (a b) i", i=128)
            bs_ = slice(b * 128, (b + 1) * 128)
            run(DVE, lambda e, bs_=bs_: e.tensor_tensor(out=scv, in0=mx[:, :, bs_],
                                                        in1=lngT.unsqueeze(2).to_broadcast([128, 8, 128]), op=ALU.mult),
                reads=[MX[b], B_setup], writes=[SCR[0], SCR[1]])
            run(DVE, lambda e: e.tensor_tensor(out=scv, in0=scv, in1=biasT, op=ALU.add), reads=[B_setup], writes=[SCR[0], SCR[1]])
            run(DVE, lambda e, bs_=bs_: e.tensor_tensor(out=ya[:, :, bs_], in0=ya[:, :, bs_], in1=scv, op=ALU.mult),
                reads=[SCR[0], SCR[1]], writes=YA)

        def u_half(half):
            wu, WU = wload(win_d[:, :, half * 512:(half + 1) * 512], (8, 512), "win")
            for gg in range(4):
                g = half * 4 + gg
                pb = fbank()
                run(PE, [lambda e, k=k, gg=gg, pb=pb, wu=wu: e.matmul(psf[:, pb, :], wu[:, k, gg * 128:(gg + 1) * 128],
                                                                       hTa[:, k, :], start=(k == 0), stop=(k == 7))
                         for k in range(8)],
                    reads=HTa + [WU], writes=[PSF[pb]])
                run(ACT, lambda e, pb=pb, g=g: e.activation(out=ya[:, g, :], in_=psf[:, pb, :], func=AF.Gelu_apprx_tanh),
                    reads=[PSF[pb]], writes=[YA[g]])

        def gate_half(col0, half, slot0):
            wg, WG = wload(win_d[:, :, col0 + half * 512:col0 + (half + 1) * 512], (8, 512), "win")
            for mm in range(4):
                m = half * 4 + mm
                pb = fbank()
                run(PE, [lambda e, k=k, mm=mm, pb=pb, wg=wg: e.matmul(psf[:, pb, :], wg[:, k, mm * 128:(mm + 1) * 128],
                                                                       hTa[:, k, :], start=(k == 0), stop=(k == 7))
                         for k in range(8)], reads=HTa + [WG], writes=[PSF[pb]])
                run(ACT, lambda e, pb=pb, m=m, slot0=slot0: e.activation(out=act[:, slot0 + m, :], in_=psf[:, pb, :],
                                                                          func=AF.Tanh, scale=0.5),
                    reads=[PSF[pb]], writes=[AC[slot0 + m]])

        v_block(0)
        u_half(0)
        v_block(1)
        u_half(1)
        sp_block(0)
        v_block(2)
        gate_half(2560, 0, 0)
        sp_block(1)
        v_block(3)
        gate_half(3584, 0, 8)
        sp_block(2)
        gate_half(2560, 1, 0)
        gate_half(3584, 1, 8)
        sp_block(3)
        for half in range(2):
            wbb, WBB = wload(wb_d[:, :, half * 512:(half + 1) * 512], (4, 512), "wb")
            wa_, WA_ = wload(wa_d[:, :, half * 512:(half + 1) * 512], (8, 512), "wa")
            for mm in range(4):
                m = half * 4 + mm
                p3, p4 = next_psf(), next_psf()
                run(PE, [lambda e, k=k, mm=mm, p3=p3, wa_=wa_: e.matmul(psf[:, p3, :], wa_[:, k, mm * 128:(mm + 1) * 128],
                                                                         ya[:, k, :], start=(k == 0), stop=(k == 7))
                         for k in range(8)], reads=YA + [WA_], writes=[PSF[p3]])
                run(DVE, lambda e, p3=p3, m=m: e.scalar_tensor_tensor(out=gts[:, 0, :], in0=act[:, m, :], scalar=1.0,
                                                                       in1=psf[:, p3, :], op0=ALU.add, op1=ALU.mult),
                    reads=[AC[m], PSF[p3]], writes=[GT[0]])
                run(PE, [lambda e, j=j, mm=mm, p4=p4, wbb=wbb, st_=st_, so_=so_: e.matmul(
                    psf[:, p4, :], wbb[:, j, mm * 128:(mm + 1) * 128], yb[:, st_, j, so_:so_ + 512],
                    start=(j == 0), stop=(j == 3)) for j in range(4)], reads=[YB[st_], WBB], writes=[PSF[p4]])
                run(DVE, lambda e, p4=p4, m=m: e.scalar_tensor_tensor(out=gts[:, 1, :], in0=act[:, 8 + m, :], scalar=1.0,
                                                                       in1=psf[:, p4, :], op0=ALU.add, op1=ALU.mult),
                    reads=[AC[8 + m], PSF[p4]], writes=[GT[1]])
                run(DVE, lambda e, m=m: e.tensor_tensor(out=mg[:, m, :], in0=gts[:, 0, :], in1=gts[:, 1, :], op=ALU.add),
                    reads=[GT[0], GT[1]], writes=[MG[m]])
        for half in range(2):
            wo_, WO_ = wload(wo_d[:, :, half * 512:(half + 1) * 512], (8, 512), "wo")
            for mm in range(4):
                m = half * 4 + mm
                pb = next_psf()
                run(PE, [lambda e, k=k, mm=mm, pb=pb, wo_=wo_: e.matmul(psf[:, pb, :], wo_[:, k, mm * 128:(mm + 1) * 128],
                                                                         mg[:, k, :], start=(k == 0), stop=(k == 7))
                         for k in range(8)], reads=MG + [WO_], writes=[PSF[pb]])
                run(ACT, lambda e, m=m, pb=pb: e.activation(out=ya[:, m, :], in_=psf[:, pb, :], func=AF.Copy, scale=0.5),
                    reads=[PSF[pb]], writes=[YA[m]])
        fm_to_x(ya, YA, xs, (0, 1, 2, 3))
        rms_scale(0, 4, xs)
        rms_scale(1, 4, xs)
        rms_transpose(0, g2T, hTb, HTb)
        rms_scale(2, 4, xs)
        rms_transpose(1, g2T, hTb, HTb)
        rms_scale(3, 4, xs)
        rms_transpose(2, g2T, hTb, HTb)
        rms_transpose(3, g2T, hTb, HTb)
        for fc in range(6):
            nfc = 4 if fc < 5 else 2
            if tt + 1 < NTILES:
                tg = 44 + 4 * ((tt + 1) % 2)
                if fc == 1:
                    rms_scale(0, tg, 1 - xs)
                if 2 <= fc <= 4:
                    rms_scale(fc - 1, tg, 1 - xs)
                    rms_transpose(fc - 2, g1T, hTa, HTa)
                if fc == 5:
                    rms_transpose(3, g1T, hTa, HTa)
            wg_, WG_ = wload(wfg_d[:, :, fc * 512:fc * 512 + nfc * 128], (8, nfc * 128), "wfg")
            wu_, WU_ = wload(wfu_d[:, :, fc * 512:fc * 512 + nfc * 128], (8, nfc * 128), "wfu")
            for ff in range(nfc):
                f = fc * 4 + ff
                p1, p2 = next_psf(), next_psf()
                run(PE, [lambda e, k=k, ff=ff, p1=p1, wg_=wg_: e.matmul(psf[:, p1, :], wg_[:, k, ff * 128:(ff + 1) * 128],
                                                                         hTb[:, k, :], start=(k == 0), stop=(k == 7))
                         for k in range(8)], reads=HTb + [WG_], writes=[PSF[p1]])
                sa = next_scr()
                run(ACT, lambda e, p1=p1, sa=sa: e.activation(out=scr[:, sa, :], in_=psf[:, p1, :], func=AF.Silu),
                    reads=[PSF[p1]], writes=[SCR[sa]])
                run(PE, [lambda e, k=k, ff=ff, p2=p2, wu_=wu_: e.matmul(psf[:, p2, :], wu_[:, k, ff * 128:(ff + 1) * 128],
                                                                         hTb[:, k, :], start=(k == 0), stop=(k == 7))
                         for k in range(8)], reads=HTb + [WU_], writes=[PSF[p2]])
                run(DVE, lambda e, f=f, p2=p2, sa=sa: e.tensor_tensor(out=act[:, f, :], in0=scr[:, sa, :], in1=psf[:, p2, :],
                                                                       op=ALU.mult),
                    reads=[SCR[sa], PSF[p2]], writes=[AC[f]])
        for m in range(8):
            wd_, WD_ = wload(wfd_d[:, m, :, :], (NF, 128), "wfd")
            pb = next_psf()
            run(PE, [lambda e, f=f, pb=pb, wd_=wd_: e.matmul(psf[:, pb, :], wd_[:, f, :], act[:, f, :],
                                                              start=(f == 0), stop=(f == NF - 1))
                     for f in range(NF)], reads=AC + [WD_], writes=[PSF[pb]])
            run(ACT, lambda e, m=m, pb=pb: e.activation(out=ya[:, m, :], in_=psf[:, pb, :], func=AF.Copy),
                reads=[PSF[pb]], writes=[YA[m]])
        fm_to_x(ya, YA, xs)
        scr_flat = scr.rearrange("p a b -> p (a b)")
        for b in range(4):
            ssq = stats[:, 32 + b:33 + b]
            if b % 2 == 0:
                odst, OB = ostage[:, 0, :], [OST[0]]
            else:
                odst, OB = scr_flat, [SCR[0], SCR[1]]
            hs = b % 2
            run(DVE, lambda e, ssq=ssq: e.memset(ssq, 0.0), writes=[STATB[b]])
            run(ACT, lambda e, b=b, hs=hs, ssq=ssq, xs=xs: e.activation(out=hb[:, hs, :], in_=xb2[:, xs, b, :], func=AF.Square,
                                                                         accum_out=ssq),
                reads=[X2[xs][b]], writes=[HB[hs], STATB[b]])
            run(ACT, lambda e, ssq=ssq: e.activation(out=ssq, in_=ssq, func=AF.Sqrt, bias=epsc, scale=1.0 / D),
                reads=[B_setup], writes=[STATB[b]])
            run(DVE, lambda e, ssq=ssq: e.reciprocal(out=ssq, in_=ssq), writes=[STATB[b]])
            run(DVE, lambda e, b=b, odst=odst, ssq=ssq, xs=xs: e.scalar_tensor_tensor(out=odst, in0=xb2[:, xs, b, :], scalar=ssq,
                                                                                in1=gfb, op0=ALU.mult, op1=ALU.mult),
                reads=[X2[xs][b], STATB[b], B_setup], writes=OB)
            B_out.w.append(run(PQ, lambda e, b=b, odst=odst, t0=t0: e.dma_start(out=out[t0 + b * 128:t0 + (b + 1) * 128, :],
                                                                      in_=odst), reads=OB))

    for t in B_out.w:
        POOL.wait(t)

    sem_cms = {n: nc.semaphore(n) for n in all_sem_names}
    sems = {n: cm.__enter__() for n, cm in sem_cms.items()}
    with nc.Block() as block:
        def replay(E):
            def f(eng):
                for o in E.ops:
                    if o[0] == "w":
                        eng.wait_ge(sems[o[1]], o[2])
                    else:
                        ins = o[1](eng)
                        if o[2] is not None:
                            ins.then_inc(sems[o[2]], o[3])
            return f
        block.tensor(replay(PE))
        block.scalar(replay(ACT))
        block.vector(replay(DVE))
        block.gpsimd(replay(POOL))
        block.sync(replay(SQ))
    for cm in sem_cms.values():
        cm.__exit__(None, None, None)
    psb_cm.__exit__(None, None, None)
    psf_cm.__exit__(None, None, None)
    arena_cm.__exit__(None, None, None)
    return nc


B_out = Buf()


def emit_s5_gen(run, Buf, E, AL, dram, psf, PSF, ident_f, B_const, outs):
    PE, ACT, DVE, POOL, SQ = E["PE"], E["ACT"], E["DVE"], E["POOL"], E["SQ"]
    G = Buf()
    GT_ = Buf()

    def ld(fn):
        run(SQ, fn, writes=[G], acc=True)

    def dv(fn):
        run(DVE, fn, writes=[G])

    def ac(fn):
        run(ACT, fn, writes=[G])

    lr2 = AL.get([32], F32)
    li2 = AL.get([32], F32)
    ldt = AL.get([32], F32)
    Bre2 = AL.get([32, 16], F32)
    Bim2 = AL.get([32, 16], F32)
    cst_re = AL.get([4, 128], F32)
    cst_im = AL.get([4, 128], F32)
    Cre2 = AL.get([32, 16], F32)
    Cim2 = AL.get([32, 16], F32)
    dcol = AL.get([32], F32)
    maskT = AL.get([128], F32)
    cpi = AL.get([2], F32)
    stg = AL.get([3, 128], F32)
    for h in range(2):
        ps = slice(64 * h, 64 * h + 64)
        ld(lambda e, ps=ps: e.dma_start(out=stg[0:32, 0, ps], in_=dram["lam_re"][:, :]))
        ld(lambda e, ps=ps: e.dma_start(out=stg[0:32, 1, ps], in_=dram["lam_im"][:, :]))
        ld(lambda e, ps=ps: e.dma_start(out=Bre2[ps, :, :], in_=dram["b_re"].rearrange("g p h -> p g h")))
        ld(lambda e, ps=ps: e.dma_start(out=Bim2[ps, :, :], in_=dram["b_im"].rearrange("g p h -> p g h")))
        cs = slice(64 * h, 64 * h + 64)
        ld(lambda e, cs=cs: e.dma_start(out=cst_re[:, :, cs], in_=dram["c_re"].rearrange("(r a) h p -> (a h) r p", a=8)))
        ld(lambda e, cs=cs: e.dma_start(out=cst_im[:, :, cs], in_=dram["c_im"].rearrange("(r a) h p -> (a h) r p", a=8)))
    ld(lambda e: e.dma_start(out=ldt, in_=dram["log_dt"].partition_broadcast(128)))
    for s in range(8):
        ld(lambda e, s=s: e.dma_start(out=stg[0:32, 2, 16 * s:16 * s + 16], in_=dram["s5_d"][:, :]))
    for i_, dst_ in enumerate((lr2, li2, dcol)):
        run(PE, lambda e, i_=i_: e.transpose(out=psf[:, 1, 0:32], in_=stg[0:32, i_, :], identity=ident_f[0:32, 0:32]),
            reads=[G, B_const], writes=[PSF[1]])
        run(DVE, lambda e, dst_=dst_: e.tensor_copy(out=dst_, in_=psf[:, 1, 0:32]), reads=[PSF[1]], writes=[G])
    ld(lambda e: e.dma_start(out=maskT, in_=dram["c_mask_toep"][:, :]))
    run(POOL, lambda e: e.memset(cpi[:, 0:1], -float(np.pi)), writes=[G], acc=True)
    run(POOL, lambda e: e.memset(cpi[:, 1:2], float(np.pi) / 2.0), writes=[G], acc=True)

    for (cst, C2) in ((cst_re, Cre2), (cst_im, Cim2)):
        for r in range(4):
            run(PE, lambda e, cst=cst, r=r: e.transpose(out=psf[:, 0, 0:128], in_=cst[:, r, :], identity=ident_f),
                reads=[G, B_const], writes=[PSF[0]])
            run(DVE, lambda e, C2=C2, r=r: e.tensor_copy(out=C2[:, r * 8:(r + 1) * 8, :],
                                                         in_=psf[:, 0, 0:128].rearrange("p (a h) -> p a h", h=16)),
                reads=[PSF[0]], writes=[G])

    def t32():
        return AL.get([32], F32)

    dt, lrd, lid, ar, ai = (t32() for _ in range(5))
    yy = t32()
    dv(lambda e: e.tensor_scalar(out=yy, in0=ldt, scalar1=0.125, scalar2=None, op0=ALU.mult))
    dv(lambda e: e.tensor_scalar(out=dt, in0=yy, scalar1=1.0 / 12.0, scalar2=1.0, op0=ALU.mult, op1=ALU.add))
    for kk in range(11, 0, -1):
        dv(lambda e: e.tensor_tensor(out=dt, in0=dt, in1=yy, op=ALU.mult))
        dv(lambda e, kk=kk: e.tensor_scalar(out=dt, in0=dt, scalar1=1.0 / kk, scalar2=1.0, op0=ALU.mult, op1=ALU.add))
    for _ in range(3):
        dv(lambda e: e.tensor_tensor(out=dt, in0=dt, in1=dt, op=ALU.mult))
    dv(lambda e: e.tensor_tensor(out=lrd, in0=lr2, in1=dt, op=ALU.mult))
    dv(lambda e: e.tensor_tensor(out=lid, in0=li2, in1=dt, op=ALU.mult))
    wr, wi, pr_, pi2, qr, qi, ur, ui, e1, e2, e3 = (t32() for _ in range(11))
    dv(lambda e: e.tensor_scalar(out=wr, in0=lrd, scalar1=1.0 / 256.0, scalar2=None, op0=ALU.mult))
    dv(lambda e: e.tensor_scalar(out=wi, in0=lid, scalar1=1.0 / 256.0, scalar2=None, op0=ALU.mult))
    dv(lambda e: e.tensor_scalar(out=pr_, in0=wr, scalar1=1.0 / 5.0, scalar2=1.0, op0=ALU.mult, op1=ALU.add))
    dv(lambda e: e.tensor_scalar(out=pi2, in0=wi, scalar1=1.0 / 5.0, scalar2=None, op0=ALU.mult))

    def cm(orr, oii, xr, xi, yr, yi):
        dv(lambda e: e.tensor_tensor(out=e1, in0=xr, in1=yr, op=ALU.mult))
        dv(lambda e: e.tensor_tensor(out=e2, in0=xi, in1=yi, op=ALU.mult))
        dv(lambda e: e.tensor_tensor(out=e3, in0=xr, in1=yi, op=ALU.mult))
        dv(lambda e: e.tensor_tensor(out=oii, in0=xi, in1=yr, op=ALU.mult))
        dv(lambda e: e.tensor_tensor(out=oii, in0=oii, in1=e3, op=ALU.add))
        dv(lambda e: e.tensor_tensor(out=orr, in0=e1, in1=e2, op=ALU.subtract))

    for dv_ in (4.0, 3.0, 2.0):
        cm(qr, qi, wr, wi, pr_, pi2)
        dv(lambda e, dv_=dv_: e.tensor_scalar(out=pr_, in0=qr, scalar1=1.0 / dv_, scalar2=1.0, op0=ALU.mult, op1=ALU.add))
        dv(lambda e, dv_=dv_: e.tensor_scalar(out=pi2, in0=qi, scalar1=1.0 / dv_, scalar2=None, op0=ALU.mult))
    cm(ur, ui, wr, wi, pr_, pi2)

    def sq_u():
        dv(lambda e: e.tensor_tensor(out=e1, in0=ur, in1=ur, op=ALU.mult))
        dv(lambda e: e.tensor_tensor(out=e2, in0=ui, in1=ui, op=ALU.mult))
        dv(lambda e: e.tensor_tensor(out=e1, in0=e1, in1=e2, op=ALU.subtract))
        dv(lambda e: e.tensor_scalar(out=e3, in0=ur, scalar1=1.0, scalar2=None, op0=ALU.add))
        dv(lambda e: e.scalar_tensor_tensor(out=ur, in0=ur, scalar=2.0, in1=e1, op0=ALU.mult, op1=ALU.add))
        dv(lambda e: e.scalar_tensor_tensor(out=ui, in0=ui, scalar=2.0, in1=e3, op0=ALU.mult, op1=ALU.mult))

    for _ in range(8):
        sq_u()
    dv(lambda e: e.tensor_scalar(out=ar, in0=ur, scalar1=1.0, scalar2=None, op0=ALU.add))
    dv(lambda e: e.tensor_copy(out=ai, in_=ui))
    den, nr, cre, cim, t1, t2 = (t32() for _ in range(6))
    dv(lambda e: e.tensor_tensor(out=den, in0=lr2, in1=lr2, op=ALU.mult))
    dv(lambda e: e.tensor_tensor(out=t1, in0=li2, in1=li2, op=ALU.mult))
    dv(lambda e: e.tensor_tensor(out=den, in0=den, in1=t1, op=ALU.add))
    dv(lambda e: e.reciprocal(out=den, in_=den))
    dv(lambda e: e.tensor_scalar(out=nr, in0=ar, scalar1=-1.0, scalar2=None, op0=ALU.add))
    dv(lambda e: e.tensor_tensor(out=t1, in0=nr, in1=lr2, op=ALU.mult))
    dv(lambda e: e.tensor_tensor(out=t2, in0=ai, in1=li2, op=ALU.mult))
    dv(lambda e: e.tensor_tensor(out=t1, in0=t1, in1=t2, op=ALU.add))
    dv(lambda e: e.tensor_tensor(out=cre, in0=t1, in1=den, op=ALU.mult))
    dv(lambda e: e.tensor_tensor(out=t1, in0=ai, in1=lr2, op=ALU.mult))
    dv(lambda e: e.tensor_tensor(out=t2, in0=nr, in1=li2, op=ALU.mult))
    dv(lambda e: e.tensor_tensor(out=t1, in0=t1, in1=t2, op=ALU.subtract))
    dv(lambda e: e.tensor_tensor(out=cim, in0=t1, in1=den, op=ALU.mult))
    bbr = AL.get([32, 16], F32)
    bbi = AL.get([32, 16], F32)
    tb = AL.get([32, 16], F32)
    X1 = AL.get([32, 16], F32)
    X2 = AL.get([32, 16], F32)
    Z1 = AL.get([32, 16], F32)
    Z2 = AL.get([32, 16], F32)

    def bc16(a):
        return a.unsqueeze(2).to_broadcast([128, 32, 16])

    dv(lambda e: e.tensor_tensor(out=bbr, in0=Bre2, in1=bc16(cre), op=ALU.mult))
    dv(lambda e: e.tensor_tensor(out=tb, in0=Bim2, in1=bc16(cim), op=ALU.mult))
    dv(lambda e: e.tensor_tensor(out=bbr, in0=bbr, in1=tb, op=ALU.subtract))
    dv(lambda e: e.tensor_tensor(out=bbi, in0=Bim2, in1=bc16(cre), op=ALU.mult))
    dv(lambda e: e.tensor_tensor(out=tb, in0=Bre2, in1=bc16(cim), op=ALU.mult))
    dv(lambda e: e.tensor_tensor(out=bbi, in0=bbi, in1=tb, op=ALU.add))
    lo, hi = slice(0, 64), slice(64, 128)
    dv(lambda e: e.tensor_copy(out=X1[lo], in_=bbr[lo]))
    dv(lambda e: e.tensor_copy(out=X1[hi], in_=bbi[hi]))
    dv(lambda e: e.tensor_scalar(out=X2[lo], in0=bbi[lo], scalar1=-1.0, scalar2=None, op0=ALU.mult))
    dv(lambda e: e.tensor_copy(out=X2[hi], in_=bbr[hi]))
    dv(lambda e: e.tensor_copy(out=Z1[lo], in_=Cre2[lo]))
    dv(lambda e: e.tensor_scalar(out=Z1[hi], in0=Cim2[hi], scalar1=-1.0, scalar2=None, op0=ALU.mult))
    dv(lambda e: e.tensor_scalar(out=Z2[lo], in0=Cim2[lo], scalar1=-1.0, scalar2=None, op0=ALU.mult))
    dv(lambda e: e.tensor_scalar(out=Z2[hi], in0=Cre2[hi], scalar1=-1.0, scalar2=None, op0=ALU.mult))
    Pr = AL.get([16, 32], F32)
    Pi_ = AL.get([16, 32], F32)
    inr, ini, m2 = t32(), t32(), t32()
    dv(lambda e: e.tensor_tensor(out=m2, in0=ar, in1=ar, op=ALU.mult))
    dv(lambda e: e.tensor_tensor(out=t1, in0=ai, in1=ai, op=ALU.mult))
    dv(lambda e: e.tensor_tensor(out=m2, in0=m2, in1=t1, op=ALU.add))
    dv(lambda e: e.reciprocal(out=m2, in_=m2))
    dv(lambda e: e.tensor_tensor(out=inr, in0=ar, in1=m2, op=ALU.mult))
    dv(lambda e: e.tensor_tensor(out=ini, in0=ai, in1=m2, op=ALU.mult))
    dv(lambda e: e.tensor_scalar(out=ini, in0=ini, scalar1=-1.0, scalar2=None, op0=ALU.mult))
    dv(lambda e: e.memset(Pr[:, 7, :], 1.0))
    dv(lambda e: e.memset(Pi_[:, 7, :], 0.0))

    def cmul(orr, oii, xr, xi, yr, yi):
        dv(lambda e: e.tensor_tensor(out=t1, in0=xr, in1=yr, op=ALU.mult))
        dv(lambda e: e.tensor_tensor(out=t2, in0=xi, in1=yi, op=ALU.mult))
        dv(lambda e: e.tensor_tensor(out=orr, in0=t1, in1=t2, op=ALU.subtract))
        dv(lambda e: e.tensor_tensor(out=t1, in0=xr, in1=yi, op=ALU.mult))
        dv(lambda e: e.tensor_tensor(out=t2, in0=xi, in1=yr, op=ALU.mult))
        dv(lambda e: e.tensor_tensor(out=oii, in0=t1, in1=t2, op=ALU.add))

    for k in range(1, 9):
        cmul(Pr[:, 7 + k, :], Pi_[:, 7 + k, :], Pr[:, 6 + k, :], Pi_[:, 6 + k, :], ar, ai)
    for k in range(1, 8):
        cmul(Pr[:, 7 - k, :], Pi_[:, 7 - k, :], Pr[:, 8 - k, :], Pi_[:, 8 - k, :], inr, ini)

    MR = AL.get([32, 128], F32)
    MRm = AL.get([32, 128], F32)
    Om = AL.get([32, 128], F32)
    O_all = outs["O_all"]
    tm = AL.get([32, 16], F32)
    tmT = AL.get([128], F32)

    tms = [tm] + [AL.get([32, 16], F32) for _ in range(3)]
    WB_ = [Buf() for _ in range(4)]
    DB_ = [Buf() for _ in range(4)]

    def wz_batch(items):
        for i, (dst, kidx, A1, A2) in enumerate(items):
            run(DVE, lambda e, i=i, kidx=kidx, A1=A1: e.tensor_tensor(out=tms[i], in0=A1, in1=bc16(Pr[:, kidx, :]), op=ALU.mult),
                reads=[G], writes=[WB_[i]])
            run(DVE, lambda e, dst=dst, kidx=kidx, A2=A2: e.tensor_tensor(out=dst, in0=A2, in1=bc16(Pi_[:, kidx, :]), op=ALU.mult),
                reads=[G], writes=[DB_[i]])
        for i, (dst, kidx, A1, A2) in enumerate(items):
            run(DVE, lambda e, i=i, dst=dst: e.tensor_tensor(out=dst, in0=dst, in1=tms[i], op=ALU.add),
                reads=[WB_[i]], writes=[DB_[i]])

    for s in range(8):
        wz_batch([(MR[:, :, 16 * s:16 * s + 16], 7 + (7 - s), X1, X2),
                  (MRm[:, :, 16 * s:16 * s + 16], 7 - s, X1, X2),
                  (Om[:, :, 16 * s:16 * s + 16], 7 + s, Z1, Z2),
                  (O_all[:, :, 16 * s:16 * s + 16], 7 + s + 1, Z1, Z2)])
    G.w = [DVE.last[0]]
    G.r = []
    R_all, T_all = outs["R_all"], outs["T_all"]
    GI = Buf()
    GI.w = list(G.w)
    GO = Buf()
    tmTs = [tmT, AL.get([128], F32)]
    GTs = [GT_, Buf()]
    for g in range(32):
        pb = g % 2
        run(PE, lambda e, g=g, pb=pb: e.transpose(out=psf[:, pb, 0:128], in_=MR[:, g, :], identity=ident_f),
            reads=[GI, B_const], writes=[PSF[pb]])
        run(ACT, lambda e, g=g, pb=pb: e.activation(out=R_all[:, g, :], in_=psf[:, pb, 0:128], func=AF.Copy),
            reads=[PSF[pb]], writes=[GO], acc=True, nowait=True)
        pt = 2 + g % 2
        run(PE, lambda e, g=g, pt=pt: e.matmul(psf[:, pt, 0:128], MRm[:, g, :], Om[:, g, :], start=True, stop=True),
            reads=[GI], writes=[PSF[pt]])
        run(DVE, lambda e, g=g, pt=pt: e.tensor_tensor(out=tmTs[g % 2], in0=psf[:, pt, 0:128], in1=maskT, op=ALU.mult),
            reads=[PSF[pt], GI], writes=[GTs[g % 2]])
        run(DVE, lambda e, g=g: e.scalar_tensor_tensor(out=T_all[:, g, :], in0=ident_f, scalar=dcol[:, g:g + 1],
                                                        in1=tmTs[g % 2], op0=ALU.mult, op1=ALU.add),
            reads=[GTs[g % 2], GI, B_const], writes=[GO], acc=True, nowait=True)
    G.w = list(G.w) + list(GO.w)
    D_all, S_f = outs["D_all"], outs["S_f"]
    p16r, p16i, p24r, p24i = (t32() for _ in range(4))
    cmul(p16r, p16i, Pr[:, 15, :], Pi_[:, 15, :], Pr[:, 15, :], Pi_[:, 15, :])
    cmul(p24r, p24i, p16r, p16i, Pr[:, 15, :], Pi_[:, 15, :])
    tmDs = [AL.get([128], F32) for _ in range(4)]
    TBs = [Buf() for _ in range(4)]
    GD = Buf()
    c2k = [t32() for _ in range(3)]
    pows = ((Pr[:, 15, :], Pi_[:, 15, :]), (p16r, p16i), (p24r, p24i))
    for k, (br_, bi_) in enumerate(pows):
        dv(lambda e, bi_=bi_, k=k: e.tensor_copy(out=c2k[k][lo], in_=bi_[lo]))
        dv(lambda e, bi_=bi_, k=k: e.tensor_scalar(out=c2k[k][hi], in0=bi_[hi], scalar1=-1.0, scalar2=None, op0=ALU.mult))
    for k, (br_, bi_) in enumerate(pows):
        for g0 in range(0, 32, 4):
            for i in range(4):
                g = g0 + i
                run(DVE, lambda e, g=g, k=k, i=i: e.tensor_scalar(out=tmDs[i], in0=S_f, scalar1=c2k[k][:, g:g + 1], scalar2=None,
                                                                  op0=ALU.mult), reads=[G, B_const], writes=[TBs[i]])
            for i in range(4):
                g = g0 + i
                run(DVE, lambda e, g=g, k=k, i=i, br_=br_: e.scalar_tensor_tensor(
                    out=D_all[:, k, g, :], in0=ident_f, scalar=br_[:, g:g + 1], in1=tmDs[i], op0=ALU.mult, op1=ALU.add),
                    reads=[TBs[i], G, B_const], writes=[GD], acc=True, nowait=True)
    for t_ in GD.w:
        DVE.wait(t_)
    AA, BB = outs["AA"], outs["BB"]
    for _ in range(5):
        sq_u()
    A8r, A8i = t32(), ui
    dv(lambda e: e.tensor_scalar(out=A8r, in0=ur, scalar1=1.0, scalar2=None, op0=ALU.add))
    dv(lambda e: e.tensor_copy(out=AA[:, 0:32], in_=A8r))
    dv(lambda e: e.tensor_copy(out=AA[:, 32:64], in_=A8r))
    dv(lambda e: e.tensor_scalar(out=BB[lo, 0:32], in0=A8i[lo], scalar1=-1.0, scalar2=None, op0=ALU.mult))
    dv(lambda e: e.tensor_copy(out=BB[hi, 0:32], in_=A8i[hi]))
    dv(lambda e: e.tensor_copy(out=BB[lo, 32:64], in_=A8i[lo]))
    dv(lambda e: e.tensor_scalar(out=BB[hi, 32:64], in0=A8i[hi], scalar1=-1.0, scalar2=None, op0=ALU.mult))
    G.w = list(G.w) + list(GD.w[-1:])
    return G


_CACHE = {}
LAST_RES = None


def _consts():
    ident = np.eye(128, dtype=np.float32)
    r = np.arange(128)
    mask_toep = (r[None, :] // 16 >= r[:, None] // 16).astype(np.float32)
    mask_ws = (r[:, None] // 64 <= r[None, :] // 64).astype(np.float32)
    return {
        "c_ident_bf": ident.astype(ml_dtypes.bfloat16),
        "c_ident_f": ident,
        "c_mask_toep": mask_toep,
        "c_mask_ws": mask_ws,
        "c_swap_bf": np.roll(ident, 64, axis=1).astype(ml_dtypes.bfloat16),
        "c_swap_f": np.roll(ident, 64, axis=1),
    }


def kernel(**inputs):
    B_out.w, B_out.r = [], []
    nc = build_program()
    x = np.ascontiguousarray(np.asarray(inputs["x"], dtype=np.float32)).reshape(8, NTOK, D)
    shared = {}
    for k, v in inputs.items():
        if k == "x":
            continue
        a = np.asarray(v, dtype=np.float32)
        if k != "norm_f_g":
            a = a[0]
        shared[k] = np.ascontiguousarray(a)
    shared.update(_consts())
    zeros = np.zeros((NTOK, D), np.float32)
    in_maps = []
    for c in range(8):
        m = dict(shared)
        m["x_own"] = x[c]
        m["x_prev"] = x[c - 1] if (c % 2 == 1) else zeros
        in_maps.append(m)
    res = run_bass_kernel_spmd(nc, in_maps, core_ids=list(range(8)))
    global LAST_RES
    LAST_RES = res.results
    outs = [np.asarray(r["out"], dtype=np.float32) for r in res.results]
    return np.stack(outs, 0).reshape(4, 8192, D)
```
